# Optimizing a Trainium2 kernel written in Bass

```python
import jax, jax.numpy as jnp
from jax import lax
import numpy as np

D_MODEL = 1024
BATCH = 32
SEQ = 2048
DEPTH = 1
DEC_BATCH = 1
DEC_SEQ = 16384
PAST_LEN = 128

D_MIX = D_MODEL
HEAD_DIM = 64
N_Q_HEADS = (D_MIX // 2) // HEAD_DIM
N_KV_HEADS = 2
Q_PER_KV = N_Q_HEADS // N_KV_HEADS
D_ATTN = N_Q_HEADS * HEAD_DIM
D_KV = N_KV_HEADS * HEAD_DIM
D_LRU = D_MIX - D_ATTN
N_LRU_BLOCKS = 8
LRU_BLOCK = D_LRU // N_LRU_BLOCKS
LRU_C = 8.0
CONV_W = 4
CONV_PAD = (2, 1)
D_IN = D_ATTN + 2 * D_KV + 2 * D_LRU
D_FF = 4 * D_MODEL
GRID_W = 64
ROPE_HALF = HEAD_DIM // 2
ROPE_THETA = 10000.0
Q_BLOCK = 128
EPS = 1e-6

kernel_name = "hymba_rglru_axial_gqa_encoder"


def rms_norm(x, g):
    xf = x.astype(jnp.float32)
    var = jnp.mean(xf * xf, axis=-1, keepdims=True)
    return (xf * lax.rsqrt(var + EPS) * g.astype(jnp.float32)).astype(x.dtype)


def axial_rope_tables(seq_len):
    rows = seq_len // GRID_W
    row_ids = jnp.repeat(jnp.arange(rows), GRID_W).astype(jnp.float32)
    col_ids = jnp.tile(jnp.arange(GRID_W), rows).astype(jnp.float32)
    inv_freq = ROPE_THETA ** (-jnp.arange(0, ROPE_HALF, 2, dtype=jnp.float32) / ROPE_HALF)
    ang_r = row_ids[:, None, None] * inv_freq
    ang_c = col_ids[:, None, None] * inv_freq
    return (jnp.cos(ang_r), jnp.sin(ang_r), jnp.cos(ang_c), jnp.sin(ang_c))


def _rotate_half(x, cos, sin):
    x1, x2 = jnp.split(x, 2, axis=-1)
    return jnp.concatenate([x1 * cos - x2 * sin, x2 * cos + x1 * sin], axis=-1)


def apply_axial_rope(x, tables):
    cos_r, sin_r, cos_c, sin_c = tables
    xf = x.astype(jnp.float32)
    x_row, x_col = jnp.split(xf, 2, axis=-1)
    out = jnp.concatenate([_rotate_half(x_row, cos_r, sin_r), _rotate_half(x_col, cos_c, sin_c)], axis=-1)
    return out.astype(x.dtype)


def bidir_gqa(q, k, v):
    B, S = q.shape[0], q.shape[1]
    n_blk = S // Q_BLOCK
    qb = (q * (HEAD_DIM ** -0.5)).reshape(B, n_blk, Q_BLOCK, N_KV_HEADS, Q_PER_KV, HEAD_DIM)
    qb = qb.transpose(1, 0, 2, 3, 4, 5)

    def one_block(q_blk):
        s = jnp.einsum('bqkgd,bskd->bkgqs', q_blk, k, preferred_element_type=jnp.float32)
        p = jax.nn.softmax(s, axis=-1).astype(v.dtype)
        return jnp.einsum('bkgqs,bskd->bqkgd', p, v)

    o = lax.map(one_block, qb)
    return o.transpose(1, 0, 2, 3, 4, 5).reshape(B, S, D_ATTN)


def centred_depthwise_conv(x, w, b):
    y = lax.conv_general_dilated(
        x, w.astype(x.dtype)[:, None, :], window_strides=(1,), padding=[CONV_PAD],
        dimension_numbers=('NWC', 'WIO', 'NWC'), feature_group_count=x.shape[-1])
    return y + b.astype(x.dtype)


def _linear_combine(left, right):
    a1, b1 = left
    a2, b2 = right
    return a1 * a2, a2 * b1 + b2


def rglru_scan(x, wa, ba, wx, bx, lam, reverse):
    B, S = x.shape[0], x.shape[1]
    xg = x.reshape(B, S, N_LRU_BLOCKS, LRU_BLOCK)
    r = jax.nn.sigmoid((jnp.einsum('bsnc,ncd->bsnd', xg, wa).reshape(B, S, D_LRU) + ba).astype(jnp.float32))
    i = jax.nn.sigmoid((jnp.einsum('bsnc,ncd->bsnd', xg, wx).reshape(B, S, D_LRU) + bx).astype(jnp.float32))
    log_a = -LRU_C * jax.nn.softplus(-lam.astype(jnp.float32)) * r
    a = jnp.exp(log_a)
    u = jnp.sqrt(-jnp.expm1(2.0 * log_a)) * (i * x.astype(jnp.float32))
    _, h = lax.associative_scan(_linear_combine, (a, u), axis=1, reverse=reverse)
    return h


def recurrent_group(x_br, y_br, conv_w, conv_b, lru_wa, lru_ba, lru_wx, lru_bx, lru_lambda):
    xc = centred_depthwise_conv(x_br, conv_w, conv_b)
    h_fwd = rglru_scan(xc, lru_wa[0], lru_ba[0], lru_wx[0], lru_bx[0], lru_lambda[0], False)
    h_bwd = rglru_scan(xc, lru_wa[1], lru_ba[1], lru_wx[1], lru_bx[1], lru_lambda[1], True)
    return ((h_fwd + h_bwd) * jax.nn.gelu(y_br.astype(jnp.float32))).astype(x_br.dtype)


def encoder_layer(x, norm_mix_g, w_in, q_norm_g, k_norm_g, conv_w, conv_b, lru_wa, lru_ba,
                  lru_wx, lru_bx, lru_lambda, w_out, norm_mlp_g, w_up, w_down):
    B, S = x.shape[0], x.shape[1]
    h = rms_norm(x, norm_mix_g)
    proj = h @ w_in
    splits = [D_ATTN, D_ATTN + D_KV, D_ATTN + 2 * D_KV, D_ATTN + 2 * D_KV + D_LRU]
    q, k, v, x_br, y_br = jnp.split(proj, splits, axis=-1)
    q = rms_norm(q.reshape(B, S, N_Q_HEADS, HEAD_DIM), q_norm_g)
    k = rms_norm(k.reshape(B, S, N_KV_HEADS, HEAD_DIM), k_norm_g)
    v = v.reshape(B, S, N_KV_HEADS, HEAD_DIM)
    tables = axial_rope_tables(S)
    attn = bidir_gqa(apply_axial_rope(q, tables), apply_axial_rope(k, tables), v)
    rec = recurrent_group(x_br, y_br, conv_w, conv_b, lru_wa, lru_ba, lru_wx, lru_bx, lru_lambda)
    x = x + jnp.concatenate([attn, rec], axis=-1) @ w_out
    h = rms_norm(x, norm_mlp_g)
    x = x + jnp.square(jax.nn.relu(h @ w_up)) @ w_down
    return x


def encoder_trunk(x, norm_mix_g, w_in, q_norm_g, k_norm_g, conv_w, conv_b, lru_wa, lru_ba,
                  lru_wx, lru_bx, lru_lambda, w_out, norm_mlp_g, w_up, w_down, norm_final_g):
    for l in range(DEPTH):
        x = encoder_layer(x, norm_mix_g[l], w_in[l], q_norm_g[l], k_norm_g[l], conv_w[l], conv_b[l],
                          lru_wa[l], lru_ba[l], lru_wx[l], lru_bx[l], lru_lambda[l], w_out[l],
                          norm_mlp_g[l], w_up[l], w_down[l])
    return rms_norm(x, norm_final_g)


def setup_inputs(seed: int = 0) -> dict:
    key = jax.random.key(seed)
    ks = jax.random.split(key, 20)
    f32 = jnp.float32

    def nrm(k, shape, scale):
        return jax.random.normal(k, shape, f32) * scale

    a0 = jax.random.uniform(ks[12], (DEPTH, 2, D_LRU), f32, minval=0.9, maxval=0.999)
    return {
        "x_prompt": nrm(ks[0], (BATCH, SEQ, D_MODEL), 1.0),
        "x_sample": nrm(ks[1], (DEC_BATCH, DEC_SEQ, D_MODEL), 1.0),
        "norm_mix_g": 1.0 + nrm(ks[2], (DEPTH, D_MODEL), 0.02),
        "w_in": nrm(ks[3], (DEPTH, D_MODEL, D_IN), D_MODEL ** -0.5),
        "q_norm_g": 1.0 + nrm(ks[4], (DEPTH, HEAD_DIM), 0.02),
        "k_norm_g": 1.0 + nrm(ks[5], (DEPTH, HEAD_DIM), 0.02),
        "conv_w": nrm(ks[6], (DEPTH, CONV_W, D_LRU), CONV_W ** -0.5),
        "conv_b": nrm(ks[7], (DEPTH, D_LRU), 0.01),
        "lru_wa": nrm(ks[8], (DEPTH, 2, N_LRU_BLOCKS, LRU_BLOCK, LRU_BLOCK), LRU_BLOCK ** -0.5),
        "lru_ba": nrm(ks[9], (DEPTH, 2, D_LRU), 0.01),
        "lru_wx": nrm(ks[10], (DEPTH, 2, N_LRU_BLOCKS, LRU_BLOCK, LRU_BLOCK), LRU_BLOCK ** -0.5),
        "lru_bx": nrm(ks[11], (DEPTH, 2, D_LRU), 0.01),
        "lru_lambda": jnp.log(a0) - jnp.log1p(-a0),
        "w_out": nrm(ks[13], (DEPTH, D_MIX, D_MODEL), D_MIX ** -0.5),
        "norm_mlp_g": 1.0 + nrm(ks[14], (DEPTH, D_MODEL), 0.02),
        "w_up": nrm(ks[15], (DEPTH, D_MODEL, D_FF), D_MODEL ** -0.5),
        "w_down": nrm(ks[16], (DEPTH, D_FF, D_MODEL), D_FF ** -0.5),
        "norm_final_g": 1.0 + nrm(ks[17], (D_MODEL,), 0.02),
    }


def reference(x_prompt, x_sample, norm_mix_g, w_in, q_norm_g, k_norm_g, conv_w, conv_b, lru_wa,
              lru_ba, lru_wx, lru_bx, lru_lambda, w_out, norm_mlp_g, w_up, w_down, norm_final_g):
    y_prompt = encoder_trunk(x_prompt, norm_mix_g, w_in, q_norm_g, k_norm_g, conv_w, conv_b, lru_wa,
                             lru_ba, lru_wx, lru_bx, lru_lambda, w_out, norm_mlp_g, w_up, w_down,
                             norm_final_g)
    y_sample = encoder_trunk(x_sample, norm_mix_g, w_in, q_norm_g, k_norm_g, conv_w, conv_b, lru_wa,
                             lru_ba, lru_wx, lru_bx, lru_lambda, w_out, norm_mlp_g, w_up, w_down,
                             norm_final_g)
    return (y_prompt, y_sample)
```

```python
import math
from contextlib import ExitStack

import numpy as np
import concourse.bass as bass
import concourse.mybir as mybir
from concourse.bass_utils import run_bass_kernel_spmd

F32 = mybir.dt.float32
BF16 = mybir.dt.bfloat16
AF = mybir.ActivationFunctionType
ALU = mybir.AluOpType
AX = mybir.AxisListType

NCORES = 8
D = 1024
T = 2048
NU = 5
DIN = 1792
DFF = 4096
EPS = 1e-6
GELU_K = math.sqrt(2.0 / math.pi)


class Op:
    __slots__ = ("q", "fn", "deps", "sem", "val", "needed", "dma")


class Prog:
    QUEUES = ("pe", "act", "dve", "pool", "sp")

    def __init__(self):
        self.ops = {q: [] for q in self.QUEUES}
        self.dma_sems = {}
        self.pending_dma = []

    def add(self, q, fn, deps=(), dma=None):
        op = Op()
        op.q, op.fn, op.dma = q, fn, dma
        ds = []
        seen = set()
        for d in deps:
            if d is None or id(d) in seen:
                continue
            seen.add(id(d))
            if d.dma is None and d.q == q and q == "pe":
                continue
            ds.append(d)
            d.needed = True
        op.deps = ds
        op.sem = ("dma:" + dma) if dma is not None else ("q:" + q)
        op.needed = dma is not None
        op.val = None
        self.ops[q].append(op)
        if dma is not None:
            self.dma_sems.setdefault(dma, []).append(op)
            self.pending_dma.append(op)
        return op

    def barrier_deps(self):
        deps = [self.ops[q][-1] for q in ("pe", "act", "dve", "pool") if self.ops[q]]
        deps += self.pending_dma
        self.pending_dma = []
        return deps

    def sem_keys(self):
        return ["q:" + q for q in self.QUEUES] + ["dma:" + k for k in self.dma_sems]

    def finalize(self):
        for q in self.QUEUES:
            c = 0
            for op in self.ops[q]:
                if op.dma is None and op.needed:
                    c += 1
                    op.val = c
        for key, lst in self.dma_sems.items():
            c = 0
            for op in lst:
                c += (1 if key == "cc" else 16)
                op.val = c

    def replay(self, q, eng, sems, final_waits=()):
        waited = {}
        for op in self.ops[q]:
            need = {}
            for d in op.deps:
                if d.val > need.get(d.sem, 0):
                    need[d.sem] = d.val
            for sk, v in need.items():
                if waited.get(sk, 0) >= v:
                    continue
                waited[sk] = v
                eng.wait_ge(sems[sk], v)
            ins = op.fn(eng)
            if op.dma is not None:
                ins.then_inc(sems[op.sem], 1 if op.dma == "cc" else 16)
            elif op.needed:
                ins.then_inc(sems[op.sem], 1)
        need = {}
        for d in final_waits:
            if d.val > need.get(d.sem, 0):
                need[d.sem] = d.val
        for sk, v in need.items():
            if waited.get(sk, 0) >= v:
                continue
            eng.wait_ge(sems[sk], v)


class Buf:
    def __init__(self, init=(), name=None):
        self.w = None
        self.r = {}
        self.init = list(init)
        self.name = name


def build_nc():
    nc = bass.Bass("TRN2", target_bir_lowering=False)
    P = Prog()

    def din(name, shape, dt=F32):
        return nc.dram_tensor(name, list(shape), dt, kind="ExternalInput").ap()

    x = din("x", [NU * T, D])
    xh = din("xh", [4, D])
    w_in = din("w_in", [D, DIN])
    w_out = din("w_out", [D, D])
    w_up = din("w_up", [D, DFF])
    w_down = din("w_down", [DFF, D])
    gvec = din("gvec", [3, D])
    gqk = din("gqk", [2, 64])
    cpk = din("cpk", [128, 4, 11])
    lwa = din("lwa", [2, 8, 64, 64])
    lwx = din("lwx", [2, 8, 64, 64])
    ropeP = din("ropeP", [128, 2, 16, 64])
    ropeS = din("ropeS", [128, 2, 16, 64])
    cmask = din("cmask", [128, 16])
    y = nc.dram_tensor("y", [NU * T, D], F32, kind="ExternalOutput").ap()

    x1s = nc.dram_tensor("x1s", [NU * T, D], F32).ap()
    w_in16 = nc.dram_tensor("w_in16", [D, DIN], BF16).ap()
    w_out16 = nc.dram_tensor("w_out16", [D, D], BF16).ap()
    w_up16 = nc.dram_tensor("w_up16", [D, DFF], BF16).ap()
    w_down16 = nc.dram_tensor("w_down16", [DFF, D], BF16).ap()
    kg_in = nc.dram_tensor("kg_in", [128, T], BF16)
    kg_out = nc.dram_tensor("kg_out", [NCORES * 128, T], BF16)
    vg_in = nc.dram_tensor("vg_in", [128, 16 * 192], BF16)
    vg_out = nc.dram_tensor("vg_out", [NCORES * 128, 16 * 192], BF16)
    sg_in = nc.dram_tensor("sg_in", [128, 16], F32)
    sg_out = nc.dram_tensor("sg_out", [NCORES * 128, 16], F32)
    q_st = nc.dram_tensor("q_st", [128, 4 * T], BF16).ap()
    xbr_st = nc.dram_tensor("xbr_st", [128, 4 * (T + 4)], BF16).ap()
    yg_st = nc.dram_tensor("yg_st", [128, 4 * T], BF16).ap()

    es = ExitStack()
    with es:
        def sbt(name, shape, dt):
            return es.enter_context(nc.sbuf_tensor(name, list(shape), dt))

        def OP(q, fn, reads=(), writes=(), dma=None, extra=()):
            deps = list(extra)
            for b in reads:
                deps.append(b.w)
                deps += b.init
            for b in writes:
                deps.append(b.w)
                deps += list(b.r.values())
                deps += b.init
            op = P.add(q, fn, deps, dma=dma)
            key = op.sem if dma is not None else q
            for b in reads:
                b.r[key] = op
            for b in writes:
                b.w = op
                b.r = {}
                b.init = []
            return op

        def MM(out, lhsT, rhs, start, stop, reads, writes):
            return OP("pe", lambda e: e.matmul(out, lhsT=lhsT, rhs=rhs, start=start, stop=stop), reads, writes)

        def TR(out, in_, reads, writes):
            return OP("pe", lambda e: e.transpose(out=out, in_=in_, identity=ident[:]), list(reads) + [b_const], writes)

        def ACTV(out, in_, func, reads, writes, bias=None, scale=None, accum_out=None):
            kw = {}
            if bias is not None:
                kw["bias"] = bias
            if scale is not None:
                kw["scale"] = scale
            if accum_out is not None:
                kw["accum_out"] = accum_out
            return OP("act", lambda e: e.activation(out=out, in_=in_, func=func, **kw), reads, writes)

        def TT(q, out, in0, in1, op, reads, writes):
            return OP(q, lambda e: e.tensor_tensor(out=out, in0=in0, in1=in1, op=op), reads, writes)

        def TS(q, out, in0, s1, s2, op0, op1, reads, writes):
            if op1 is None:
                return OP(q, lambda e: e.tensor_scalar(out=out, in0=in0, scalar1=s1, scalar2=None, op0=op0), reads, writes)
            return OP(q, lambda e: e.tensor_scalar(out=out, in0=in0, scalar1=s1, scalar2=s2, op0=op0, op1=op1), reads, writes)

        def STT(out, in0, scalar, in1, op0, op1, reads, writes):
            return OP("dve", lambda e: e.scalar_tensor_tensor(out=out, in0=in0, scalar=scalar, in1=in1, op0=op0, op1=op1), reads, writes)

        def CP(q, out, in_, reads, writes):
            if q == "act":
                return OP("act", lambda e: e.activation(out=out, in_=in_, func=AF.Copy), reads, writes)
            return OP(q, lambda e: e.tensor_copy(out=out, in_=in_), reads, writes)

        def MS(q, ap, val, writes):
            return OP(q, lambda e: e.memset(ap, val), (), writes)

        dma_ctr = [0]

        def DMA(q, out, in_, reads, writes, sem=None, extra=()):
            assert sem is not None
            return OP(q, lambda e: e.dma_start(out=out, in_=in_), reads, writes, dma=sem, extra=extra)

        ps = es.enter_context(nc.psum_tensor("ps", [128, 8, 512], F32))
        psT = ps[:, 6:8, :].bitcast(BF16)
        b_ps = [Buf() for _ in range(8)]
        b_psT = [b_ps[6], b_ps[7]]

        ident = sbt("ident", [128, 128], BF16)
        ident32 = sbt("ident32", [128, 128], F32)
        selA = sbt("selA", [128, 128], F32)
        selB = sbt("selB", [128, 128], F32)
        gbc = sbt("gbc", [128, 3, D], F32)
        gqk_bc = sbt("gqk_bc", [128, 2, 64], F32)
        gqk_sw = sbt("gqk_sw", [128, 2, 64], F32)
        cp = sbt("cp", [128, 4, 11], F32)
        hb = sbt("hb", [128, 4, 4], F32)
        hc = sbt("hc", [128, 4, 2], F32)
        tmpc = sbt("tmpc", [128, 4, 2], F32)
        nbound = sbt("nbound", [128, 1], F32)
        mxq = sbt("mxq", [128, 2], F32)
        cm = sbt("cm", [128, 16], F32)
        negh = sbt("negh", [128, 16], F32)
        summ = sbt("summ", [128, 16], F32)
        sth = sbt("sth", [128, 4, 2, 2], F32)
        gath = sbt("gath", [128, NCORES, 16], F32)
        carry = sbt("carry", [128, 4, 2], F32)
        ctmp = sbt("ctmp", [128, NCORES, 16], F32)
        b_const = Buf()
        b_rtab = Buf()
        b_summ = Buf()
        b_carry = Buf()

        RW = 48704
        LIM1 = 42560
        R = sbt("R", [128, RW], F32)
        rpos = [0]
        rlim = [LIM1]
        rtab = R[:, LIM1:LIM1 + 4096].rearrange("p (a t d) -> p a t d", a=4, t=16)
        gw = R[:, LIM1 + 4096:LIM1 + 5120].bitcast(BF16).rearrange("p (a b) -> p a b", a=16)

        def carve(shape, dt, init, name=None):
            n = 1
            for s in shape[1:]:
                n *= s
            words = n if dt == F32 else (n + 1) // 2
            a = rpos[0]
            rpos[0] += words
            assert rpos[0] <= rlim[0], ("overlay overflow", rpos[0], rlim[0])
            ap = R[:, a:a + words]
            if dt != F32:
                ap = ap.bitcast(dt)[:, 0:n]
            if len(shape) == 3:
                ap = ap.rearrange("p (a b) -> p a b", a=shape[1])
            elif len(shape) == 4:
                ap = ap.rearrange("p (a b c) -> p a b c", a=shape[1], b=shape[2])
            return ap, Buf(init, name)

        wops = []
        for wi, (src, dst, rows, cols) in enumerate(((w_in, w_in16, D, DIN), (w_out, w_out16, D, D), (w_up, w_up16, D, DFF), (w_down, w_down16, DFF, D))):
            lst = []
            for r0 in range(0, rows, 512):
                for c0 in range(0, cols, 1024):
                    c1 = min(cols, c0 + 1024)
                    lst.append(DMA("pool", dst[r0:r0 + 512, c0:c1], src[r0:r0 + 512, c0:c1], (), (), sem="wc%d" % wi))
            wops.append([lst[-1]])
        w_in16_ready, w_out16_ready, w_up16_ready, w_down16_ready = wops

        MS("pool", ident32[:], 0.0, [b_const])
        OP("pool", lambda e: e.affine_select(out=ident32[:], in_=ident32[:], compare_op=ALU.not_equal, fill=1.0, base=0,
                                            pattern=[[-1, 128]], channel_multiplier=1), (), [b_const])
        CP("dve", ident[:], ident32[:], [b_const], [b_const])
        MS("pool", selA[:], 0.0, [b_const])
        MS("pool", selB[:], 0.0, [b_const])
        MS("pool", selA[64:65, 0:64], 1.0, [b_const])
        MS("pool", selB[0:1, 64:128], 1.0, [b_const])
        MS("pool", negh[:], -0.5, [b_const])
        MS("pool", gw[:], 0.0, [b_const])
        for di in range(2):
            for gi, wsrc in enumerate((lwa, lwx)):
                i0 = (di * 2 + gi) * 4
                DMA("pool", gw[0:64, i0:i0 + 4, 0:64], wsrc[di, 0::2].rearrange("n c d -> c n d"), (), [b_const], sem="c0")
                DMA("pool", gw[64:128, i0:i0 + 4, 64:128], wsrc[di, 1::2].rearrange("n c d -> c n d"), (), [b_const], sem="c0")
        for i in range(3):
            DMA("sp", gbc[:, i, :], gvec[i, :].partition_broadcast(128), (), [b_const], sem="c1")
        for i in range(2):
            DMA("sp", gqk_bc[:, i, :], gqk[i, :].partition_broadcast(128), (), [b_const], sem="c1")
        DMA("sp", cp[:], cpk[:, :, :], (), [b_const], sem="c1")
        DMA("sp", cm[:], cmask[:, :], (), [b_const], sem="c1")
        for qi in range(2):
            CP("dve", gqk_sw[:, qi, :].rearrange("p (a f j) -> p a f j", a=2, f=2, j=16),
               gqk_bc[:, qi, :].rearrange("p (a f j) -> p a f j", a=2, f=2, j=16)[:, :, ::-1, :], [b_const], [b_const])
        TS("dve", hb[:], cp[:, :, 5:9], 0.5, None, ALU.mult, None, [b_const], [b_const])
        ACTV(tmpc[:], cp[:, :, 9:11], AF.Exp, [b_const], [b_const], scale=-1.0)
        ACTV(tmpc[:], tmpc[:], AF.Ln, [b_const], [b_const], bias=1.0)
        TS("dve", hc[:], tmpc[:], -4.0, None, ALU.mult, None, [b_const], [b_const])
        OP("dve", lambda e: e.tensor_reduce(out=mxq[:], in_=gqk_bc[:], op=ALU.max, axis=AX.X, apply_absolute_value=True), [b_const], [b_const])
        TT("dve", nbound[:], mxq[:, 0:1], mxq[:, 1:2], ALU.mult, [b_const], [b_const])
        TS("dve", nbound[:], nbound[:], -8.0, None, ALU.mult, None, [b_const], [b_const])

        def v5(ap, h):
            return ap.rearrange("p (h a f j) -> p h a f j", h=h, a=2, f=2, j=16)

        def load_rope(src):
            rpos_save = rpos[0]
            rpos[0] = 0
            st, b_st = carve([128, 2, 16, 64], F32, P.barrier_deps(), "st")
            DMA("sp", st, src[:, :, :, :], (), [b_st], sem="st")
            for qi in range(2):
                g = gqk_bc[:, qi, :].rearrange("p (a f j) -> p a f j", a=2, f=2, j=16)
                gb = g.unsqueeze(1).to_broadcast([128, 16, 2, 2, 16])
                gsw = gqk_sw[:, qi, :].rearrange("p (a f j) -> p a f j", a=2, f=2, j=16).unsqueeze(1).to_broadcast([128, 16, 2, 2, 16])
                cview = st[:, 0].rearrange("p t (a f j) -> p t a f j", a=2, f=2, j=16)
                sview = st[:, 1].rearrange("p t (a f j) -> p t a f j", a=2, f=2, j=16)
                oc = rtab[:, 2 * qi].rearrange("p t (a f j) -> p t a f j", a=2, f=2, j=16)
                osn = rtab[:, 2 * qi + 1].rearrange("p t (a f j) -> p t a f j", a=2, f=2, j=16)
                TT("dve", oc, cview, gb, ALU.mult, [b_st, b_const], [b_rtab])
                TT("dve", osn, sview, gsw, ALU.mult, [b_st, b_const], [b_rtab])
            TS("dve", rtab[:, 0:2], rtab[:, 0:2], 0.125, None, ALU.mult, None, [b_rtab], [b_rtab])
            rpos[0] = rpos_save

        def norm_block(xb, b_xb, ntile, g_idx, hbuf, b_h, ss, rs, b_ss):
            for i in range(ntile):
                ACTV(hbuf[:, i, :], xb[:, i, :], AF.Square, [b_xb], [b_h, b_ss], accum_out=ss[:, i:i + 1])
            TS("dve", ss[:, 0:ntile], ss[:, 0:ntile], 1.0 / D, EPS, ALU.mult, ALU.add, [b_ss], [b_ss])
            TT("pool", rs[:, 0:ntile], ss[:, 0:ntile], negh[:, 0:ntile], ALU.pow, [b_ss, b_const], [b_ss])
            for i in range(ntile):
                STT(hbuf[:, i, :], xb[:, i, :], rs[:, i:i + 1], gbc[:, g_idx, :], ALU.mult, ALU.mult, [b_xb, b_ss, b_const], [b_h])

        tr_ctr = [0]

        def transpose_tiles(hbuf, b_h, ntile, hT, b_hT):
            for i in range(ntile):
                s = tr_ctr[0] % 2
                tr_ctr[0] += 1
                for kc in range(8):
                    TR(psT[:, s, kc * 128:(kc + 1) * 128], hbuf[:, i, kc * 128:(kc + 1) * 128], [b_h], [b_psT[s]])
                CP("act", hT[:, :, i * 128:(i + 1) * 128], psT[:, s, :].rearrange("p (k t) -> p k t", k=8), [b_psT[s]], [b_hT])

        POS_A = 0
        XW = T + 4
        POS_B = POS_A + 4 * XW // 2
        POS_C = POS_B + 4 * T // 2
        POS_D = POS_C + 4 * T // 2
        POS_E = POS_D + 4 * T // 2
        POS_F = POS_E + T // 2
        POS_S = POS_F + 16 * 192 // 2
        dwt = R[:, 47680:48704].bitcast(BF16).rearrange("p (a b) -> p a b", a=16)
        for ct_ in range(4):
            for k_ in range(4):
                TS("dve", dwt[:, ct_ * 4 + k_, :], ident32[:], cp[:, ct_, k_:k_ + 1], None, ALU.mult, None, [b_const], [b_const])
        deferred = []

        def run_unit(u, mode):
            sample = mode != "prompt"
            bar = P.barrier_deps()
            rpos[0] = POS_A
            xbr, b_xbr = carve([128, 4, XW], BF16, bar)
            yg, b_yg = carve([128, 4, T], BF16, bar)
            recT, b_rec = carve([128, 4, T], BF16, bar)
            qT, b_qT = carve([128, 4, T], BF16, bar)
            kT, b_kT = carve([128, T], BF16, bar)
            Vb, b_V = carve([128, 16, 192], BF16, bar)
            assert rpos[0] == POS_S

            def dbl(shape, dt, init, n=2):
                out = []
                for _ in range(n):
                    out.append(carve(shape, dt, init))
                return [o[0] for o in out], [o[1] for o in out]

            if mode == "sample_post":
                DMA("sp", xbr.rearrange("p c t -> p (c t)"), xbr_st, (), [b_xbr], sem="ld_x", extra=stash_ops)
                DMA("sp", yg.rearrange("p c t -> p (c t)"), yg_st, (), [b_yg], sem="ld_y", extra=stash_ops)
            else:
                rpos[0] = POS_C
                wa, b_wa = carve([128, 8, 1024], BF16, bar)
                DMA("sp", wa, w_in16[:, 768:1792].rearrange("(k p) n -> p k n", p=128), (), [b_wa], sem="wa", extra=w_in16_ready)
                xbs, b_xbs = dbl([128, 4, D], F32, bar)
                hbs, b_hbs = dbl([128, 4, D], BF16, bar)
                hTs, b_hTs = dbl([128, 8, 512], BF16, bar)
                sss, b_sss = dbl([128, 8], F32, bar)
                ysbs, b_ysbs = dbl([128, 512], F32, bar)
                y2s, b_y2s = dbl([128, 512], F32, bar)
                pctr = 0
                if sample:
                    MS("pool", xbs[1][:, 0, :], 0.0, [b_xbs[1]])
                    DMA("sp", xbs[1][0:4, 0, :], xh[:, :], (), [b_xbs[1]], sem="xb1")
                    norm_block(xbs[1], b_xbs[1], 1, 0, hbs[1], b_hbs[1], sss[1], sss[1][:, 4:8], b_sss[1])
                    transpose_tiles(hbs[1], b_hbs[1], 1, hTs[1], b_hTs[1])
                    for ct in range(4):
                        bk = pctr % 6
                        pctr += 1
                        for kc in range(8):
                            MM(ps[:, bk, 0:128], wa[:, kc, ct * 128:(ct + 1) * 128], hTs[1][:, kc, 0:128], kc == 0, kc == 7, [b_wa, b_hTs[1]], [b_ps[bk]])
                        CP("dve", xbr[:, ct, 0:2], ps[:, bk, 0:2], [b_ps[bk]], [b_xbr])
                        CP("dve", xbr[:, ct, T + 2:T + 4], ps[:, bk, 2:4], [b_ps[bk]], [b_xbr])
                else:
                    MS("pool", xbr[:, :, 0:2], 0.0, [b_xbr])
                    MS("pool", xbr[:, :, T + 2:T + 4], 0.0, [b_xbr])

                def load_norm(blk, g_idx=0):
                    k = blk % 2
                    t0 = u * T + blk * 512
                    DMA("sp", xbs[k], x[t0:t0 + 512, :].rearrange("(n p) d -> p n d", p=128), (), [b_xbs[k]], sem="xb%d" % k)
                    norm_block(xbs[k], b_xbs[k], 4, g_idx, hbs[k], b_hbs[k], sss[k], sss[k][:, 4:8], b_sss[k])

                load_norm(0)
                yctr = 0
                for blk in range(4):
                    k = blk % 2
                    transpose_tiles(hbs[k], b_hbs[k], 4, hTs[k], b_hTs[k])
                    if blk + 1 < 4:
                        load_norm(blk + 1)
                    for ft in range(8):
                        bk = pctr % 6
                        pctr += 1
                        for kc in range(8):
                            MM(ps[:, bk, :], wa[:, kc, ft * 128:(ft + 1) * 128], hTs[k][:, kc, :], kc == 0, kc == 7, [b_wa, b_hTs[k]], [b_ps[bk]])
                        if ft < 4:
                            CP("dve", xbr[:, ft, 2 + blk * 512:2 + (blk + 1) * 512], ps[:, bk, :], [b_ps[bk]], [b_xbr])
                        else:
                            ct = ft - 4
                            yk = yctr % 2
                            yctr += 1
                            ysb, b_ysb, y2, b_y2 = ysbs[yk], b_ysbs[yk], y2s[yk], b_y2s[yk]
                            CP("act", ysb, ps[:, bk, :], [b_ps[bk]], [b_ysb])
                            ACTV(y2, ps[:, bk, :], AF.Square, [b_ps[bk]], [b_y2])
                            TS("dve", y2, y2, 0.044715, 1.0, ALU.mult, ALU.add, [b_y2], [b_y2])
                            TT("dve", y2, y2, ysb, ALU.mult, [b_y2, b_ysb], [b_y2])
                            ACTV(y2, y2, AF.Tanh, [b_y2], [b_y2], scale=GELU_K)
                            STT(yg[:, ct, blk * 512:(blk + 1) * 512], y2, 1.0, ysb, ALU.add, ALU.mult, [b_y2, b_ysb], [b_yg])
                if mode == "sample_pre":
                    stash_ops.append(DMA("sp", xbr_st, xbr.rearrange("p c t -> p (c t)"), [b_xbr], (), sem="st_x"))
                    stash_ops.append(DMA("sp", yg_st, yg.rearrange("p c t -> p (c t)"), [b_yg], (), sem="st_y"))

            bar = P.barrier_deps()
            b_rec.init = list(bar)
            rpos[0] = POS_D
            xcs, b_xcs = dbl([128, T], F32, bar)
            xc16s, b_xc16s = dbl([128, T], BF16, bar)
            avs, b_avs = dbl([128, 2, 1024], F32, bar)
            ivs, b_ivs = dbl([128, 2, 1024], F32, bar)
            svs, b_svs = dbl([128, 2, 1024], F32, bar)
            hf, b_hf = carve([128, T], F32, bar)
            hbk, b_hbk = carve([128, T], F32, bar)

            cvc = [0]

            def conv(ct):
                xc, b_xc = xcs[ct % 2], b_xcs[ct % 2]
                for blk in range(4):
                    bk = 6 + cvc[0] % 2
                    cvc[0] += 1
                    for k in range(4):
                        MM(ps[:, bk, :], dwt[:, ct * 4 + k, :], xbr[:, ct, blk * 512 + k:blk * 512 + k + 512], k == 0, k == 3, [b_const, b_xbr], [b_ps[bk]])
                    ACTV(xc[:, blk * 512:(blk + 1) * 512], ps[:, bk, :], AF.Identity, [b_ps[bk], b_const], [b_xc], bias=cp[:, ct, 4:5])
                CP("pool", xc16s[ct % 2], xc, [b_xc], [b_xc16s[ct % 2]])

            gctr = 0
            grp = 0
            conv(0)
            for ct in range(4):
                xc, b_xc, xc16, b_xc16 = xcs[ct % 2], b_xcs[ct % 2], xc16s[ct % 2], b_xc16s[ct % 2]
                for di in range(2):
                    s_ = grp % 2
                    grp += 1
                    av, b_av, iv, b_iv, sv, b_sv = avs[s_], b_avs[s_], ivs[s_], b_ivs[s_], svs[s_], b_svs[s_]
                    halves = (0, 1) if di == 0 else (1, 0)
                    for hf_i in halves:
                        tsl = slice(hf_i * 1024, (hf_i + 1) * 1024)
                        slots = []
                        for gi in range(2):
                            sl = gctr % 3
                            gctr += 1
                            slots.append(sl)
                            wsel = gw[:, (di * 2 + gi) * 4 + ct, :]
                            for sb_ in range(2):
                                MM(ps[:, 2 * sl + sb_, :], wsel, xc16[:, hf_i * 1024 + sb_ * 512: hf_i * 1024 + (sb_ + 1) * 512], True, True,
                                   [b_const, b_xc16], [b_ps[2 * sl + sb_]])
                        zr = ps[:, 2 * slots[0]:2 * slots[0] + 2, :].rearrange("p a b -> p (a b)")
                        zi = ps[:, 2 * slots[1]:2 * slots[1] + 2, :].rearrange("p a b -> p (a b)")
                        br = [b_ps[2 * slots[0]], b_ps[2 * slots[0] + 1]]
                        bi = [b_ps[2 * slots[1]], b_ps[2 * slots[1] + 1]]
                        if mode == "sample_pre":
                            ACTV(av[:, hf_i, :], zr, AF.Tanh, br + [b_const], [b_av, b_summ], bias=hb[:, ct, di:di + 1], scale=0.5,
                                 accum_out=sth[:, ct, di, hf_i:hf_i + 1])
                        else:
                            ACTV(av[:, hf_i, :], zr, AF.Tanh, br + [b_const], [b_av], bias=hb[:, ct, di:di + 1], scale=0.5)
                        ACTV(av[:, hf_i, :], av[:, hf_i, :], AF.Exp, [b_av, b_const], [b_av], bias=hc[:, ct, di:di + 1], scale=hc[:, ct, di:di + 1])
                        ACTV(iv[:, hf_i, :], zi, AF.Tanh, bi + [b_const], [b_iv], bias=hb[:, ct, 2 + di:3 + di], scale=0.5)
                        TT("pool", sv[:, hf_i, :], av[:, hf_i, :], av[:, hf_i, :], ALU.mult, [b_av], [b_sv])
                        STT(iv[:, hf_i, :], iv[:, hf_i, :], 1.0, xc[:, tsl], ALU.add, ALU.mult, [b_iv, b_xc], [b_iv])
                    if di == 0 and ct + 1 < 4:
                        conv(ct + 1)
                    ACTV(sv, sv, AF.Sqrt, [b_sv], [b_sv], bias=1.0, scale=-1.0)
                    for n_, hf_i in enumerate(halves):
                        tsl = slice(hf_i * 1024, (hf_i + 1) * 1024)
                        STT(iv[:, hf_i, :], iv[:, hf_i, :], 0.5, sv[:, hf_i, :], ALU.mult, ALU.mult, [b_iv, b_sv], [b_iv])
                        hdst, b_hd = (hf, b_hf) if di == 0 else (hbk, b_hbk)
                        if n_ == 0:
                            if mode == "sample_post":
                                init, rd = carry[:, ct, di:di + 1], [b_carry]
                            else:
                                init, rd = 0.0, []
                        else:
                            init = hdst[:, 1023:1024] if di == 0 else hdst[:, 1024:1025]
                            rd = []
                        if di == 0:
                            o_, a_, u_ = hdst[:, tsl], av[:, hf_i, :], iv[:, hf_i, :]
                        else:
                            o_, a_, u_ = hdst[:, tsl][:, ::-1], av[:, hf_i, :][:, ::-1], iv[:, hf_i, :][:, ::-1]
                        OP("dve", (lambda o, a__, u__, i_: (lambda e: e.tensor_tensor_scan(out=o, data0=a__, data1=u__, initial=i_, op0=ALU.mult, op1=ALU.add)))(
                            o_, a_, u_, init), [b_av, b_iv] + rd, [b_hd])
                if mode == "sample_pre":
                    for di in range(2):
                        c0 = ct * 4 + 2 * di
                        TT("dve", summ[:, c0:c0 + 1], sth[:, ct, di, 0:1], sth[:, ct, di, 1:2], ALU.add, [b_summ], [b_summ])
                        TS("dve", summ[:, c0:c0 + 1], summ[:, c0:c0 + 1], float(T), None, ALU.add, None, [b_summ], [b_summ])
                        ACTV(summ[:, c0:c0 + 1], summ[:, c0:c0 + 1], AF.Exp, [b_summ, b_const], [b_summ], scale=hc[:, ct, di:di + 1])
                    CP("dve", summ[:, ct * 4 + 1:ct * 4 + 2], hf[:, T - 1:T], [b_hf], [b_summ])
                    CP("dve", summ[:, ct * 4 + 3:ct * 4 + 4], hbk[:, 0:1], [b_hbk], [b_summ])
                else:
                    TT("pool", hf, hf, hbk, ALU.add, [b_hf, b_hbk], [b_hf])
                    STT(recT[:, ct, :], hf, 0.5, yg[:, ct, :], ALU.mult, ALU.mult, [b_hf, b_yg], [b_rec])

            bar = P.barrier_deps()
            for b_ in (b_qT, b_kT, b_V):
                b_.init = list(bar)
            if mode == "sample_post":
                DMA("sp", qT.rearrange("p c t -> p (c t)"), q_st, (), [b_qT], sem="ld_q", extra=stash_ops)
            else:
                rpos[0] = POS_A
                hbs, b_hbs = dbl([128, 4, D], BF16, bar)
                assert rpos[0] <= POS_C
                rpos[0] = POS_S
                xbs, b_xbs = dbl([128, 4, D], F32, bar)
                wq, b_wq = carve([128, 8, 768], BF16, bar)
                DMA("sp", wq, w_in16[:, 0:768].rearrange("(k p) n -> p k n", p=128), (), [b_wq], sem="wq", extra=w_in16_ready)
                hTs, b_hTs = dbl([128, 8, 512], BF16, bar)
                sss, b_sss = dbl([128, 8], F32, bar)
                sqs, b_sqs = dbl([128, 640], F32, bar, 3)
                t1s, b_t1s = dbl([128, 640], F32, bar, 3)
                t2s, b_t2s = dbl([128, 640], F32, bar, 3)
                qk16s, b_qk16s = dbl([128, 640], BF16, bar, 3)
                s10s, b_s10s = dbl([128, 24], F32, bar, 3)
                MS("pool", Vb, 0.0, [b_V])
                MS("pool", Vb[:, :, 64:65], 1.0, [b_V])
                pctr = 0

                def mm_qkv(tile):
                    k = (tile // 4) % 2
                    i = tile % 4
                    sl = tile % 3
                    bq, bkv = 2 * sl, 2 * sl + 1
                    for kc in range(8):
                        MM(ps[:, bq, :], hTs[k][:, kc, i * 128:(i + 1) * 128], wq[:, kc, 0:512], kc == 0, kc == 7, [b_hTs[k], b_wq], [b_ps[bq]])
                    for kc in range(8):
                        MM(ps[:, bkv, 0:256], hTs[k][:, kc, i * 128:(i + 1) * 128], wq[:, kc, 512:768], kc == 0, kc == 7, [b_hTs[k], b_wq], [b_ps[bkv]])

                def chain_qkv(tile):
                    d_ = tile % 3
                    sq, b_sq, t1, b_t1, t2, b_t2 = sqs[d_], b_sqs[d_], t1s[d_], b_t1s[d_], t2s[d_], b_t2s[d_]
                    qk16, b_qk16, s10, b_s10 = qk16s[d_], b_qk16s[d_], s10s[d_], b_s10s[d_]
                    r10 = s10[:, 12:22]
                    sl = tile % 3
                    bq, bkv = 2 * sl, 2 * sl + 1
                    qkp = ps[:, bq:bq + 2, :].rearrange("p a b -> p (a b)")[:, 0:640]
                    bb = [b_ps[bq], b_ps[bkv]]
                    ACTV(sq, qkp, AF.Square, bb, [b_sq])
                    OP("dve", (lambda o, i_: (lambda e: e.tensor_reduce(out=o, in_=i_, op=ALU.add, axis=AX.X)))(
                        s10[:, 0:10], sq.rearrange("p (h d) -> p h d", d=64)), [b_sq], [b_s10])
                    TS("dve", s10[:, 0:10], s10[:, 0:10], 1.0 / 64, EPS, ALU.mult, ALU.add, [b_s10], [b_s10])
                    TT("pool", r10, s10[:, 0:10], negh[:, 0:10], ALU.pow, [b_s10, b_const], [b_s10])
                    for (lo, nh, tq) in ((0, 8, 0), (512, 2, 2)):
                        xin = qkp[:, lo:lo + nh * 64]
                        gc = rtab[:, tq, tile, :].rearrange("p (a f j) -> p a f j", a=2, f=2, j=16).unsqueeze(1).to_broadcast([128, nh, 2, 2, 16])
                        gs = rtab[:, tq + 1, tile, :].rearrange("p (a f j) -> p a f j", a=2, f=2, j=16).unsqueeze(1).to_broadcast([128, nh, 2, 2, 16])
                        TT("dve", v5(t1[:, lo:lo + nh * 64], nh), v5(xin, nh), gc, ALU.mult, bb + [b_rtab], [b_t1])
                        TT("dve", v5(t2[:, lo:lo + nh * 64], nh), v5(xin, nh)[:, :, :, ::-1, :], gs, ALU.mult, bb + [b_rtab], [b_t2])
                    TT("pool", t1, t1, t2, ALU.add, [b_t1, b_t2], [b_t1])
                    TT("dve", qk16.rearrange("p (h d) -> p h d", d=64), t1.rearrange("p (h d) -> p h d", d=64),
                       r10.unsqueeze(2).to_broadcast([128, 10, 64]), ALU.mult, [b_t1, b_s10], [b_qk16])
                    CP("dve", Vb[:, tile, :].rearrange("p (a c) -> p a c", c=64)[:, 0::2, :],
                       ps[:, bkv, 128:256].rearrange("p (a c) -> p a c", c=64), [b_ps[bkv]], [b_V])
                    s = tr_ctr[0] % 2
                    tr_ctr[0] += 1
                    for j in range(5):
                        TR(psT[:, s, j * 128:(j + 1) * 128], qk16[:, j * 128:(j + 1) * 128], [b_qk16], [b_psT[s]])
                    CP("act", qT[:, :, tile * 128:(tile + 1) * 128], psT[:, s, 0:512].rearrange("p (k t) -> p k t", k=4), [b_psT[s]], [b_qT])
                    CP("act", kT[:, tile * 128:(tile + 1) * 128], psT[:, s, 512:640], [b_psT[s]], [b_kT])

                load_norm(0)
                for blk in range(4):
                    k = blk % 2
                    transpose_tiles(hbs[k], b_hbs[k], 4, hTs[k], b_hTs[k])
                    if blk + 1 < 4:
                        load_norm(blk + 1)
                    mm_qkv(blk * 4)
                    mm_qkv(blk * 4 + 1)
                    for i in range(4):
                        tile = blk * 4 + i
                        if i + 2 < 4:
                            mm_qkv(tile + 2)
                        chain_qkv(tile)


            if mode == "sample_pre":
                o1 = DMA("pool", kg_in.ap(), kT, [b_kT], (), sem="g0")
                o2 = DMA("pool", vg_in.ap(), Vb.rearrange("p t c -> p (t c)"), [b_V], (), sem="g0")
                o3 = DMA("pool", sg_in.ap(), summ[:], [b_summ], (), sem="g0")
                stash_ops.append(DMA("sp", q_st, qT.rearrange("p c t -> p (c t)"), [b_qT], (), sem="st_q"))

                ccs = []
                for ci, (a_in, a_out) in enumerate(((kg_in, kg_out), (vg_in, vg_out), (sg_in, sg_out))):
                    op_ = P.add("pool", (lambda i_, o_: (lambda e: e.collective_compute(
                        "AllGather", ALU.bypass, replica_groups=[list(range(NCORES))], ins=[i_.ap().opt()], outs=[o_.ap().opt()])))(a_in, a_out),
                        [o1, o2, o3] + ccs, dma="cc")
                    ccs.append(op_)
                cc_ops.extend(ccs)
                return

            bar = P.barrier_deps()
            rpos[0] = POS_A
            attT, b_att = carve([128, 4, T], BF16, bar)
            assert rpos[0] <= POS_C
            if sample:
                rpos[0] = POS_S
                ksrc, b_ks = carve([128, NCORES * T], BF16, bar)
                vsrc, b_vs = carve([128, NCORES * 16, 192], BF16, bar)
                for r in range(NCORES):
                    DMA("sp", ksrc[:, r * T:(r + 1) * T], kg_out.ap()[r * 128:(r + 1) * 128, :], (), [b_ks], sem="kvk", extra=cc_ops)
                    DMA("sp", vsrc[:, r * 16:(r + 1) * 16, :].rearrange("p t c -> p (t c)"), vg_out.ap()[r * 128:(r + 1) * 128, :], (), [b_vs], sem="kvv", extra=cc_ops)
                nkt = NCORES * 16
            else:
                rpos[0] = POS_S
                ksrc, b_ks, vsrc, b_vs, nkt = kT, b_kT, Vb, b_V, 16
            pt, b_pt = [], []
            for i in range(3):
                a_, b_ = carve([128, 2, 512], BF16, bar)
                pt.append(a_)
                b_pt.append(b_)
            Asb, b_A = carve([128, 512], F32, bar)
            Bsb, b_B = carve([128, 512], F32, bar)
            rinv, b_rinv = carve([128, 512], F32, bar)
            MS("pool", Asb, 0.0, [b_A])
            tiles = [(j, qb, kt) for j in range(4) for qb in range(4) for kt in range(nkt)]
            NTL = len(tiles)

            def emit_S(i):
                j, qb, kt = tiles[i]
                sl, pi = i % 2, i % 3
                qsl = slice(qb * 512, (qb + 1) * 512)
                ksl = slice(kt * 128, (kt + 1) * 128)
                MM(ps[:, 2 * sl, :], ksrc[0:64, ksl], qT[0:64, j, qsl], True, True, [b_ks, b_qT], [b_ps[2 * sl]])
                MM(ps[:, 2 * sl + 1, :], ksrc[64:128, ksl], qT[64:128, j, qsl], True, True, [b_ks, b_qT], [b_ps[2 * sl + 1]])
                ACTV(pt[pi], ps[:, 2 * sl:2 * sl + 2, :], AF.Exp, [b_ps[2 * sl], b_ps[2 * sl + 1], b_const], [b_pt[pi]], bias=nbound[:, 0:1])

            def emit_PV(i):
                j, qb, kt = tiles[i]
                pi = i % 3
                MM(ps[0:65, 4, :], vsrc[:, kt, 0:65], pt[pi][:, 0, :], kt == 0, kt == nkt - 1, [b_vs, b_pt[pi]], [b_ps[4]])
                MM(ps[:, 5, :], vsrc[:, kt, 64:192], pt[pi][:, 1, :], kt == 0, kt == nkt - 1, [b_vs, b_pt[pi]], [b_ps[5]])

            def emit_fin(j, qb):
                qsl = slice(qb * 512, (qb + 1) * 512)
                MM(ps[:, 6, :], selA[:], Asb, True, False, [b_const, b_A], [b_ps[6]])
                MM(ps[:, 6, :], selB[:], Bsb, False, True, [b_const, b_B], [b_ps[6]])
                OP("dve", (lambda o, i_: (lambda e: e.reciprocal(out=o, in_=i_)))(rinv, ps[:, 6, :]), [b_ps[6]], [b_rinv])
                TT("pool", attT[0:64, j, qsl], Asb[0:64, :], rinv[0:64, :], ALU.mult, [b_A, b_rinv], [b_att])
                TT("pool", attT[64:128, j, qsl], Bsb[64:128, :], rinv[64:128, :], ALU.mult, [b_B, b_rinv], [b_att])

            emit_S(0)
            pend = None
            for i in range(NTL):
                if i + 1 < NTL:
                    emit_S(i + 1)
                emit_PV(i)
                j, qb, kt = tiles[i]
                if pend is not None and (i >= pend[2] or kt == nkt - 1):
                    emit_fin(pend[0], pend[1])
                    pend = None
                if kt == nkt - 1:
                    CP("dve", Asb[0:65, :], ps[0:65, 4, :], [b_ps[4]], [b_A])
                    CP("dve", Bsb, ps[:, 5, :], [b_ps[5]], [b_B])
                    pend = (j, qb, i + 3)
            if pend is not None:
                emit_fin(pend[0], pend[1])

            bar = P.barrier_deps()
            rpos[0] = POS_S
            wo, b_wo = carve([128, 8, D], BF16, bar)
            DMA("sp", wo, w_out16.rearrange("(k p) n -> p k n", p=128), (), [b_wo], sem="wo", extra=w_out16_ready)
            xbs, b_xbs = dbl([128, 4, D], F32, bar)
            xo, b_xo = dbl([128, D], F32, bar)

            def load_x(blk):
                k = blk % 2
                t0 = u * T + blk * 512
                DMA("sp", xbs[k], x[t0:t0 + 512, :].rearrange("(n p) d -> p n d", p=128), (), [b_xbs[k]], sem="xb%d" % k)

            pctr = 0
            load_x(0)
            for blk in range(4):
                t0 = u * T + blk * 512
                if blk + 1 < 4:
                    load_x(blk + 1)
                for i in range(4):
                    tsl = slice(blk * 512 + i * 128, blk * 512 + (i + 1) * 128)
                    sl = pctr % 3
                    pctr += 1
                    for half in range(2):
                        bk = 2 * sl + half
                        for kc in range(8):
                            lhs = attT[:, kc, tsl] if kc < 4 else recT[:, kc - 4, tsl]
                            MM(ps[:, bk, :], lhs, wo[:, kc, half * 512:(half + 1) * 512], kc == 0, kc == 7, [b_att, b_rec, b_wo], [b_ps[bk]])
                    o = xo[pctr % 2]
                    bo = b_xo[pctr % 2]
                    TT("dve", o, ps[:, 2 * sl:2 * sl + 2, :].rearrange("p a b -> p (a b)"), xbs[blk % 2][:, i, :], ALU.add,
                       [b_ps[2 * sl], b_ps[2 * sl + 1], b_xbs[blk % 2]], [bo])
                    r0 = t0 + i * 128
                    x1_writes.append(DMA("sp", x1s[r0:r0 + 128, :], o, [bo], (), sem="xo%d" % (pctr % 2)))
            return

        x1_writes = []
        cc_ops = []
        stash_ops = []
        load_rope(ropeS)
        run_unit(4, "sample_pre")
        load_rope(ropeP)
        for u in range(4):
            run_unit(u, "prompt")
        bar = P.barrier_deps()
        b_g = Buf(bar)
        DMA("sp", gath[:], sg_out.ap().rearrange("(r p) x -> p r x", p=128), (), [b_g], sem="gath", extra=cc_ops)
        gv = gath[:].rearrange("p r (c k) -> p r c k", k=4)
        cv = ctmp[:].rearrange("p r (c k) -> p r c k", k=4)
        for di in range(2):
            mk = cm[:, di * 8:(di + 1) * 8].unsqueeze(2).to_broadcast([128, NCORES, 4])
            TS("dve", cv[:, :, :, 2 * di], gv[:, :, :, 2 * di], -1.0, None, ALU.add, None, [b_g], [b_g])
            TT("dve", cv[:, :, :, 2 * di], cv[:, :, :, 2 * di], mk, ALU.mult, [b_g, b_const], [b_g])
            TS("dve", cv[:, :, :, 2 * di], cv[:, :, :, 2 * di], 1.0, None, ALU.add, None, [b_g], [b_g])
            TT("dve", cv[:, :, :, 2 * di + 1], gv[:, :, :, 2 * di + 1], mk, ALU.mult, [b_g, b_const], [b_g])
            MS("dve", carry[:, :, di], 0.0, [b_carry])
            order = range(NCORES) if di == 0 else range(NCORES - 1, -1, -1)
            for r in order:
                TT("dve", carry[:, :, di], carry[:, :, di], cv[:, r, :, 2 * di], ALU.mult, [b_g, b_carry], [b_carry])
                TT("dve", carry[:, :, di], carry[:, :, di], cv[:, r, :, 2 * di + 1], ALU.add, [b_g, b_carry], [b_carry])
        run_unit(4, "sample_post")

        rpos[0] = 0
        rlim[0] = RW
        bar = P.barrier_deps()
        wu, b_wu = carve([128, 8, DFF], BF16, bar)
        wd, b_wd = carve([128, 32, D], BF16, bar)
        for kc in range(8):
            DMA("sp", wu[:, kc, :], w_up16[kc * 128:(kc + 1) * 128, :], (), [b_wu], sem="wu", extra=w_up16_ready)
        for fc in range(0, 32, 8):
            DMA("sp", wd[:, fc:fc + 8, :], w_down16[fc * 128:(fc + 8) * 128, :].rearrange("(k p) n -> p k n", p=128), (), [b_wd], sem="wd",
                extra=w_down16_ready)
        aT, b_aT = carve([128, 32, 512], BF16, bar)
        hT, b_hT = carve([128, 8, 512], BF16, bar)
        xt, b_xt, hb2, b_hb2, rl, b_rl, xr, b_xr = [], [], [], [], [], [], [], []
        xt0, b_xt0 = carve([128, D], F32, bar)
        hb20, b_hb20 = carve([128, D], BF16, bar)
        for i in range(2):
            xt.append(xt0); b_xt.append(b_xt0)
            hb2.append(hb20); b_hb2.append(b_hb20)
            a_, b_ = carve([128, 512], F32, bar); rl.append(a_); b_rl.append(b_)
            a_, b_ = carve([128, D], F32, bar); xr.append(a_); b_xr.append(b_)
        yo, b_yo = carve([128, D], F32, bar)
        sn, b_sn = carve([128, 16], F32, bar)
        out_writes = []
        NB2 = NU * T // 512
        cnt = {"n": 0, "u": 0, "f": 0}

        def p2_norm(blk):
            t0 = blk * 512
            for i in range(4):
                k = cnt["n"] % 2
                cnt["n"] += 1
                r0 = t0 + i * 128
                DMA("sp", xt[k], x1s[r0:r0 + 128, :], (), [b_xt[k]], sem="xt0", extra=[x1_writes[r0 // 128]])
                ACTV(hb2[k], xt[k], AF.Square, [b_xt[k]], [b_hb2[k], b_sn], accum_out=sn[:, 0:1])
                TS("dve", sn[:, 0:1], sn[:, 0:1], 1.0 / D, EPS, ALU.mult, ALU.add, [b_sn], [b_sn])
                TT("pool", sn[:, 1:2], sn[:, 0:1], negh[:, 0:1], ALU.pow, [b_sn, b_const], [b_sn])
                STT(hb2[k], xt[k], sn[:, 1:2], gbc[:, 1, :], ALU.mult, ALU.mult, [b_xt[k], b_sn, b_const], [b_hb2[k]])
                s_ = tr_ctr[0] % 2
                tr_ctr[0] += 1
                for kc in range(8):
                    TR(psT[:, s_, kc * 128:(kc + 1) * 128], hb2[k][:, kc * 128:(kc + 1) * 128], [b_hb2[k]], [b_psT[s_]])
                CP("act", hT[:, :, i * 128:(i + 1) * 128], psT[:, s_, :].rearrange("p (k t) -> p k t", k=8), [b_psT[s_]], [b_hT])

        def p2_up(blk):
            for ff in range(32):
                bk = cnt["u"] % 2
                cnt["u"] += 1
                for kc in range(8):
                    MM(ps[:, bk, :], wu[:, kc, ff * 128:(ff + 1) * 128], hT[:, kc, :], kc == 0, kc == 7, [b_wu, b_hT], [b_ps[bk]])
                ACTV(rl[bk], ps[:, bk, :], AF.Relu, [b_ps[bk]], [b_rl[bk]])
                TT("dve", aT[:, ff, :], ps[:, bk, :], rl[bk], ALU.mult, [b_ps[bk], b_rl[bk]], [b_aT])

        def p2_down(blk, pair):
            for i in range(2):
                ti = pair * 2 + i
                for half in range(2):
                    bk = 2 + 2 * i + half
                    for ff in range(32):
                        MM(ps[:, bk, :], aT[:, ff, ti * 128:(ti + 1) * 128], wd[:, ff, half * 512:(half + 1) * 512], ff == 0, ff == 31, [b_aT, b_wd], [b_ps[bk]])

        def p2_final(blk, pair):
            t0 = blk * 512
            for i in range(2):
                ti = pair * 2 + i
                bk = 2 + 2 * i
                k = cnt["f"] % 2
                cnt["f"] += 1
                r0 = t0 + ti * 128
                DMA("sp", xr[k], x1s[r0:r0 + 128, :], (), [b_xr[k]], sem="xr%d" % k, extra=[x1_writes[r0 // 128]])
                TT("dve", xr[k], ps[:, bk:bk + 2, :].rearrange("p a b -> p (a b)"), xr[k], ALU.add, [b_ps[bk], b_ps[bk + 1], b_xr[k]], [b_xr[k]])
                ACTV(yo, xr[k], AF.Square, [b_xr[k]], [b_yo, b_sn], accum_out=sn[:, 4:5])
                TS("dve", sn[:, 4:5], sn[:, 4:5], 1.0 / D, EPS, ALU.mult, ALU.add, [b_sn], [b_sn])
                TT("pool", sn[:, 5:6], sn[:, 4:5], negh[:, 0:1], ALU.pow, [b_sn, b_const], [b_sn])
                STT(yo, xr[k], sn[:, 5:6], gbc[:, 2, :], ALU.mult, ALU.mult, [b_xr[k], b_sn, b_const], [b_yo])
                out_writes.append(DMA("sp", y[r0:r0 + 128, :], yo, [b_yo], (), sem="yo"))

        p2_norm(0)
        for blk in range(NB2):
            p2_up(blk)
            p2_down(blk, 0)
            if blk + 1 < NB2:
                p2_norm(blk + 1)
            p2_final(blk, 0)
            p2_down(blk, 1)
            p2_final(blk, 1)

        P.finalize()
        sems = {k: es.enter_context(nc.semaphore(k.replace(":", "_"))) for k in P.sem_keys()}
        with nc.Block() as block:
            @block.sync
            def _(e):
                P.replay("sp", e, sems, out_writes)

            @block.tensor
            def _(e):
                P.replay("pe", e, sems)

            @block.vector
            def _(e):
                P.replay("dve", e, sems)

            @block.scalar
            def _(e):
                P.replay("act", e, sems)

            @block.gpsimd
            def _(e):
                P.replay("pool", e, sems)
    return nc


def _rope_table(pos0):
    t = pos0 + np.arange(T)
    row = (t // 64).astype(np.float32)
    col = (t % 64).astype(np.float32)
    inv = (10000.0 ** (-np.arange(0, 32, 2, dtype=np.float32) / 32)).astype(np.float32)
    ang = np.stack([row[:, None] * inv[None, :], col[:, None] * inv[None, :]], axis=1).astype(np.float32)
    c = np.cos(ang).astype(np.float32)
    s = np.sin(ang).astype(np.float32)
    C = np.stack([c, c], axis=2)
    S = np.stack([-s, s], axis=2)
    tab = np.stack([C.reshape(T, 64), S.reshape(T, 64)], axis=0)
    tab = tab.reshape(2, 16, 128, 64).transpose(2, 0, 1, 3)
    return np.ascontiguousarray(tab, dtype=np.float32)


_NC_CACHE = {}


def kernel(x_prompt, x_sample, norm_mix_g, w_in, q_norm_g, k_norm_g, conv_w, conv_b, lru_wa, lru_ba,
           lru_wx, lru_bx, lru_lambda, w_out, norm_mlp_g, w_up, w_down, norm_final_g):
    f = lambda a: np.ascontiguousarray(np.asarray(a), dtype=np.float32)
    x_prompt, x_sample = f(x_prompt), f(x_sample)
    w_in0 = f(w_in)[0]
    perm_heads = [4 * e + j for j in range(4) for e in range(2)]
    qcols = np.concatenate([np.arange(h * 64, (h + 1) * 64) for h in perm_heads])
    w_in_l = np.ascontiguousarray(np.concatenate([w_in0[:, qcols], w_in0[:, 512:]], axis=1))
    w_out0 = f(w_out)[0]
    w_out_l = np.ascontiguousarray(np.concatenate([w_out0[qcols, :], w_out0[512:, :]], axis=0))
    gvec = np.stack([f(norm_mix_g)[0], f(norm_mlp_g)[0], f(norm_final_g)], axis=0)
    gqk = np.stack([f(q_norm_g)[0], f(k_norm_g)[0]], axis=0)
    chan = np.concatenate([f(conv_w)[0], f(conv_b), f(lru_ba)[0], f(lru_bx)[0], f(lru_lambda)[0]], axis=0)
    cpk = np.ascontiguousarray(chan.reshape(11, 4, 128).transpose(2, 1, 0))
    ropeP = _rope_table(0)
    xs = x_sample[0]
    in_maps = []
    for c in range(NCORES):
        xc = np.concatenate([x_prompt[4 * c:4 * c + 4].reshape(4 * T, D), xs[c * T:(c + 1) * T]], axis=0)
        xh = np.zeros((4, D), np.float32)
        if c > 0:
            xh[0:2] = xs[c * T - 2:c * T]
        if c < NCORES - 1:
            xh[2] = xs[(c + 1) * T]
        cmask = np.zeros((128, 16), np.float32)
        cmask[:, 0:8] = (np.arange(8) < c).astype(np.float32)[None, :]
        cmask[:, 8:16] = (np.arange(8) > c).astype(np.float32)[None, :]
        in_maps.append({
            "x": np.ascontiguousarray(xc), "xh": xh, "w_in": w_in_l, "w_out": w_out_l, "w_up": f(w_up)[0], "w_down": f(w_down)[0],
            "gvec": gvec, "gqk": gqk, "cpk": cpk, "lwa": f(lru_wa)[0], "lwx": f(lru_wx)[0],
            "ropeP": ropeP, "ropeS": _rope_table(c * T), "cmask": cmask,
        })
    if "nc" not in _NC_CACHE:
        _NC_CACHE["nc"] = build_nc()
    res = run_bass_kernel_spmd(_NC_CACHE["nc"], in_maps, core_ids=list(range(NCORES)))
    y_prompt = np.empty((32, T, D), np.float32)
    y_sample = np.empty((1, NCORES * T, D), np.float32)
    for c in range(NCORES):
        yc = res.results[c]["y"]
        y_prompt[4 * c:4 * c + 4] = yc[:4 * T].reshape(4, T, D)
        y_sample[0, c * T:(c + 1) * T] = yc[4 * T:]
    return (y_prompt, y_sample)
```

```python
import math
from contextlib import ExitStack

import numpy as np
import concourse.bass as bass
import concourse.mybir as mybir
from concourse.bass_utils import run_bass_kernel_spmd

F32 = mybir.dt.float32
BF16 = mybir.dt.bfloat16
AF = mybir.ActivationFunctionType
ALU = mybir.AluOpType
AX = mybir.AxisListType

NCORES = 8
D = 1024
T = 2048
NU = 5
DIN = 1792
DFF = 4096
EPS = 1e-6
GELU_K = math.sqrt(2.0 / math.pi)


class Op:
    __slots__ = ("q", "fn", "deps", "sem", "val", "needed", "dma")


class Prog:
    QUEUES = ("pe", "act", "dve", "pool", "sp")

    def __init__(self):
        self.ops = {q: [] for q in self.QUEUES}
        self.dma_sems = {}
        self.pending_dma = []

    def add(self, q, fn, deps=(), dma=None):
        op = Op()
        op.q, op.fn, op.dma = q, fn, dma
        ds = []
        seen = set()
        for d in deps:
            if d is None or id(d) in seen:
                continue
            seen.add(id(d))
            if d.dma is None and d.q == q and q == "pe":
                continue
            ds.append(d)
            d.needed = True
        op.deps = ds
        op.sem = ("dma:" + dma) if dma is not None else ("q:" + q)
        op.needed = dma is not None
        op.val = None
        self.ops[q].append(op)
        if dma is not None:
            self.dma_sems.setdefault(dma, []).append(op)
            self.pending_dma.append(op)
        return op

    def barrier_deps(self):
        deps = [self.ops[q][-1] for q in ("pe", "act", "dve", "pool") if self.ops[q]]
        deps += self.pending_dma
        self.pending_dma = []
        return deps

    def sem_keys(self):
        return ["q:" + q for q in self.QUEUES] + ["dma:" + k for k in self.dma_sems]

    def finalize(self):
        for q in self.QUEUES:
            c = 0
            for op in self.ops[q]:
                if op.dma is None and op.needed:
                    c += 1
                    op.val = c
        for key, lst in self.dma_sems.items():
            c = 0
            for op in lst:
                c += (1 if key == "cc" else 16)
                op.val = c

    def replay(self, q, eng, sems, final_waits=()):
        waited = {}
        for op in self.ops[q]:
            need = {}
            for d in op.deps:
                if d.val > need.get(d.sem, 0):
                    need[d.sem] = d.val
            for sk, v in need.items():
                if waited.get(sk, 0) >= v:
                    continue
                waited[sk] = v
                eng.wait_ge(sems[sk], v)
            ins = op.fn(eng)
            if op.dma is not None:
                ins.then_inc(sems[op.sem], 1 if op.dma == "cc" else 16)
            elif op.needed:
                ins.then_inc(sems[op.sem], 1)
        need = {}
        for d in final_waits:
            if d.val > need.get(d.sem, 0):
                need[d.sem] = d.val
        for sk, v in need.items():
            if waited.get(sk, 0) >= v:
                continue
            eng.wait_ge(sems[sk], v)


class Buf:
    def __init__(self, init=(), name=None):
        self.w = None
        self.r = {}
        self.init = list(init)
        self.name = name


def build_nc():
    nc = bass.Bass("TRN2", target_bir_lowering=False)
    P = Prog()

    def din(name, shape, dt=F32):
        return nc.dram_tensor(name, list(shape), dt, kind="ExternalInput").ap()

    x = din("x", [NU * T, D])
    xh = din("xh", [4, D])
    w_in = din("w_in", [D, DIN])
    w_out = din("w_out", [D, D])
    w_up = din("w_up", [D, DFF])
    w_down = din("w_down", [DFF, D])
    gvec = din("gvec", [3, D])
    gqk = din("gqk", [2, 64])
    cpk = din("cpk", [128, 4, 11])
    lwa = din("lwa", [2, 8, 64, 64])
    lwx = din("lwx", [2, 8, 64, 64])
    ropeP = din("ropeP", [128, 2, 16, 64])
    ropeS = din("ropeS", [128, 2, 16, 64])
    cmask = din("cmask", [128, 16])
    y = nc.dram_tensor("y", [NU * T, D], F32, kind="ExternalOutput").ap()

    x1s = nc.dram_tensor("x1s", [NU * T, D], F32).ap()
    w_in16 = nc.dram_tensor("w_in16", [D, DIN], BF16).ap()
    w_out16 = nc.dram_tensor("w_out16", [D, D], BF16).ap()
    w_up16 = nc.dram_tensor("w_up16", [D, DFF], BF16).ap()
    w_down16 = nc.dram_tensor("w_down16", [DFF, D], BF16).ap()
    kg_in = nc.dram_tensor("kg_in", [128, T], BF16)
    kg_out = nc.dram_tensor("kg_out", [NCORES * 128, T], BF16)
    vg_in = nc.dram_tensor("vg_in", [128, 16 * 192], BF16)
    vg_out = nc.dram_tensor("vg_out", [NCORES * 128, 16 * 192], BF16)
    sg_in = nc.dram_tensor("sg_in", [128, 16], F32)
    sg_out = nc.dram_tensor("sg_out", [NCORES * 128, 16], F32)
    q_st = nc.dram_tensor("q_st", [128, 4 * T], BF16).ap()
    xbr_st = nc.dram_tensor("xbr_st", [128, 4 * (T + 4)], BF16).ap()
    yg_st = nc.dram_tensor("yg_st", [128, 4 * T], BF16).ap()

    es = ExitStack()
    with es:
        def sbt(name, shape, dt):
            return es.enter_context(nc.sbuf_tensor(name, list(shape), dt))

        def OP(q, fn, reads=(), writes=(), dma=None, extra=()):
            deps = list(extra)
            for b in reads:
                deps.append(b.w)
                deps += b.init
            for b in writes:
                deps.append(b.w)
                deps += list(b.r.values())
                deps += b.init
            op = P.add(q, fn, deps, dma=dma)
            key = op.sem if dma is not None else q
            for b in reads:
                b.r[key] = op
            for b in writes:
                b.w = op
                b.r = {}
                b.init = []
            return op

        def MM(out, lhsT, rhs, start, stop, reads, writes):
            return OP("pe", lambda e: e.matmul(out, lhsT=lhsT, rhs=rhs, start=start, stop=stop), reads, writes)

        def TR(out, in_, reads, writes):
            return OP("pe", lambda e: e.transpose(out=out, in_=in_, identity=ident[:]), list(reads) + [b_const], writes)

        def ACTV(out, in_, func, reads, writes, bias=None, scale=None, accum_out=None):
            kw = {}
            if bias is not None:
                kw["bias"] = bias
            if scale is not None:
                kw["scale"] = scale
            if accum_out is not None:
                kw["accum_out"] = accum_out
            return OP("act", lambda e: e.activation(out=out, in_=in_, func=func, **kw), reads, writes)

        def TT(q, out, in0, in1, op, reads, writes):
            return OP(q, lambda e: e.tensor_tensor(out=out, in0=in0, in1=in1, op=op), reads, writes)

        def TS(q, out, in0, s1, s2, op0, op1, reads, writes):
            if op1 is None:
                return OP(q, lambda e: e.tensor_scalar(out=out, in0=in0, scalar1=s1, scalar2=None, op0=op0), reads, writes)
            return OP(q, lambda e: e.tensor_scalar(out=out, in0=in0, scalar1=s1, scalar2=s2, op0=op0, op1=op1), reads, writes)

        def STT(out, in0, scalar, in1, op0, op1, reads, writes):
            return OP("dve", lambda e: e.scalar_tensor_tensor(out=out, in0=in0, scalar=scalar, in1=in1, op0=op0, op1=op1), reads, writes)

        def CP(q, out, in_, reads, writes):
            if q == "act":
                return OP("act", lambda e: e.activation(out=out, in_=in_, func=AF.Copy), reads, writes)
            return OP(q, lambda e: e.tensor_copy(out=out, in_=in_), reads, writes)

        def MS(q, ap, val, writes):
            return OP(q, lambda e: e.memset(ap, val), (), writes)

        dma_ctr = [0]

        def DMA(q, out, in_, reads, writes, sem=None, extra=()):
            assert sem is not None
            return OP(q, lambda e: e.dma_start(out=out, in_=in_), reads, writes, dma=sem, extra=extra)

        ps = es.enter_context(nc.psum_tensor("ps", [128, 8, 512], F32))
        psT = ps[:, 6:8, :].bitcast(BF16)
        b_ps = [Buf() for _ in range(8)]
        b_psT = [b_ps[6], b_ps[7]]

        ident = sbt("ident", [128, 128], BF16)
        ident32 = sbt("ident32", [128, 128], F32)
        selA = sbt("selA", [128, 128], F32)
        selB = sbt("selB", [128, 128], F32)
        gbc = sbt("gbc", [128, 3, D], F32)
        gqk_bc = sbt("gqk_bc", [128, 2, 64], F32)
        gqk_sw = sbt("gqk_sw", [128, 2, 64], F32)
        cp = sbt("cp", [128, 4, 11], F32)
        hb = sbt("hb", [128, 4, 4], F32)
        hc = sbt("hc", [128, 4, 2], F32)
        tmpc = sbt("tmpc", [128, 4, 2], F32)
        nbound = sbt("nbound", [128, 1], F32)
        mxq = sbt("mxq", [128, 2], F32)
        cm = sbt("cm", [128, 16], F32)
        negh = sbt("negh", [128, 16], F32)
        summ = sbt("summ", [128, 16], F32)
        sth = sbt("sth", [128, 4, 2, 2], F32)
        gath = sbt("gath", [128, NCORES, 16], F32)
        carry = sbt("carry", [128, 4, 2], F32)
        ctmp = sbt("ctmp", [128, NCORES, 16], F32)
        b_const = Buf()
        b_rtab = Buf()
        b_summ = Buf()
        b_carry = Buf()

        RW = 48704
        LIM1 = 42560
        R = sbt("R", [128, RW], F32)
        rpos = [0]
        rlim = [LIM1]
        rtab = R[:, LIM1:LIM1 + 4096].rearrange("p (a t d) -> p a t d", a=4, t=16)
        gw = R[:, LIM1 + 4096:LIM1 + 5120].bitcast(BF16).rearrange("p (a b) -> p a b", a=16)

        def carve(shape, dt, init, name=None):
            n = 1
            for s in shape[1:]:
                n *= s
            words = n if dt == F32 else (n + 1) // 2
            a = rpos[0]
            rpos[0] += words
            assert rpos[0] <= rlim[0], ("overlay overflow", rpos[0], rlim[0])
            ap = R[:, a:a + words]
            if dt != F32:
                ap = ap.bitcast(dt)[:, 0:n]
            if len(shape) == 3:
                ap = ap.rearrange("p (a b) -> p a b", a=shape[1])
            elif len(shape) == 4:
                ap = ap.rearrange("p (a b c) -> p a b c", a=shape[1], b=shape[2])
            return ap, Buf(init, name)

        MS("pool", ident32[:], 0.0, [b_const])
        OP("pool", lambda e: e.affine_select(out=ident32[:], in_=ident32[:], compare_op=ALU.not_equal, fill=1.0, base=0,
                                            pattern=[[-1, 128]], channel_multiplier=1), (), [b_const])
        CP("dve", ident[:], ident32[:], [b_const], [b_const])
        MS("pool", selA[:], 0.0, [b_const])
        MS("pool", selB[:], 0.0, [b_const])
        MS("pool", selA[64:65, 0:64], 1.0, [b_const])
        MS("pool", selB[0:1, 64:128], 1.0, [b_const])
        MS("pool", negh[:], -0.5, [b_const])
        MS("pool", gw[:], 0.0, [b_const])
        for di in range(2):
            for gi, wsrc in enumerate((lwa, lwx)):
                i0 = (di * 2 + gi) * 4
                DMA("pool", gw[0:64, i0:i0 + 4, 0:64], wsrc[di, 0::2].rearrange("n c d -> c n d"), (), [b_const], sem="c0")
                DMA("pool", gw[64:128, i0:i0 + 4, 64:128], wsrc[di, 1::2].rearrange("n c d -> c n d"), (), [b_const], sem="c0")
        for i in range(3):
            DMA("sp", gbc[:, i, :], gvec[i, :].partition_broadcast(128), (), [b_const], sem="c1")
        for i in range(2):
            DMA("sp", gqk_bc[:, i, :], gqk[i, :].partition_broadcast(128), (), [b_const], sem="c1")
        DMA("sp", cp[:], cpk[:, :, :], (), [b_const], sem="c1")
        DMA("sp", cm[:], cmask[:, :], (), [b_const], sem="c1")
        wops = []
        for wi, (src, dst, rows, cols) in enumerate(((w_in, w_in16, D, DIN), (w_out, w_out16, D, D), (w_up, w_up16, D, DFF), (w_down, w_down16, DFF, D))):
            lst = []
            for r0 in range(0, rows, 512):
                for c0 in range(0, cols, 1024):
                    c1 = min(cols, c0 + 1024)
                    lst.append(DMA("pool", dst[r0:r0 + 512, c0:c1], src[r0:r0 + 512, c0:c1], (), (), sem="wc%d" % wi))
            wops.append([lst[-1]])
        w_in16_ready, w_out16_ready, w_up16_ready, w_down16_ready = wops
        P.pending_dma = [d for d in P.pending_dma if not d.dma.startswith("wc")]

        for qi in range(2):
            CP("dve", gqk_sw[:, qi, :].rearrange("p (a f j) -> p a f j", a=2, f=2, j=16),
               gqk_bc[:, qi, :].rearrange("p (a f j) -> p a f j", a=2, f=2, j=16)[:, :, ::-1, :], [b_const], [b_const])
        TS("dve", hb[:], cp[:, :, 5:9], 0.5, None, ALU.mult, None, [b_const], [b_const])
        ACTV(tmpc[:], cp[:, :, 9:11], AF.Exp, [b_const], [b_const], scale=-1.0)
        ACTV(tmpc[:], tmpc[:], AF.Ln, [b_const], [b_const], bias=1.0)
        TS("dve", hc[:], tmpc[:], -4.0, None, ALU.mult, None, [b_const], [b_const])
        OP("dve", lambda e: e.tensor_reduce(out=mxq[:], in_=gqk_bc[:], op=ALU.max, axis=AX.X, apply_absolute_value=True), [b_const], [b_const])
        TT("dve", nbound[:], mxq[:, 0:1], mxq[:, 1:2], ALU.mult, [b_const], [b_const])
        TS("dve", nbound[:], nbound[:], -8.0, None, ALU.mult, None, [b_const], [b_const])

        def v5(ap, h):
            return ap.rearrange("p (h a f j) -> p h a f j", h=h, a=2, f=2, j=16)

        def load_rope(src):
            rpos_save = rpos[0]
            rpos[0] = 0
            st, b_st = carve([128, 2, 16, 64], F32, P.barrier_deps(), "st")
            DMA("sp", st, src[:, :, :, :], (), [b_st], sem="st")
            for qi in range(2):
                g = gqk_bc[:, qi, :].rearrange("p (a f j) -> p a f j", a=2, f=2, j=16)
                gb = g.unsqueeze(1).to_broadcast([128, 16, 2, 2, 16])
                gsw = gqk_sw[:, qi, :].rearrange("p (a f j) -> p a f j", a=2, f=2, j=16).unsqueeze(1).to_broadcast([128, 16, 2, 2, 16])
                cview = st[:, 0].rearrange("p t (a f j) -> p t a f j", a=2, f=2, j=16)
                sview = st[:, 1].rearrange("p t (a f j) -> p t a f j", a=2, f=2, j=16)
                oc = rtab[:, 2 * qi].rearrange("p t (a f j) -> p t a f j", a=2, f=2, j=16)
                osn = rtab[:, 2 * qi + 1].rearrange("p t (a f j) -> p t a f j", a=2, f=2, j=16)
                TT("dve", oc, cview, gb, ALU.mult, [b_st, b_const], [b_rtab])
                TT("dve", osn, sview, gsw, ALU.mult, [b_st, b_const], [b_rtab])
            TS("dve", rtab[:, 0:2], rtab[:, 0:2], 0.125, None, ALU.mult, None, [b_rtab], [b_rtab])
            rpos[0] = rpos_save

        def norm_block(xb, b_xb, ntile, g_idx, hbuf, b_h, ss, rs, b_ss):
            for i in range(ntile):
                ACTV(hbuf[:, i, :], xb[:, i, :], AF.Square, [b_xb], [b_h, b_ss], accum_out=ss[:, i:i + 1])
            TS("dve", ss[:, 0:ntile], ss[:, 0:ntile], 1.0 / D, EPS, ALU.mult, ALU.add, [b_ss], [b_ss])
            TT("pool", rs[:, 0:ntile], ss[:, 0:ntile], negh[:, 0:ntile], ALU.pow, [b_ss, b_const], [b_ss])
            for i in range(ntile):
                STT(hbuf[:, i, :], xb[:, i, :], rs[:, i:i + 1], gbc[:, g_idx, :], ALU.mult, ALU.mult, [b_xb, b_ss, b_const], [b_h])

        tr_ctr = [0]

        def transpose_tiles(hbuf, b_h, ntile, hT, b_hT):
            for i in range(ntile):
                s = tr_ctr[0] % 2
                tr_ctr[0] += 1
                for kc in range(8):
                    TR(psT[:, s, kc * 128:(kc + 1) * 128], hbuf[:, i, kc * 128:(kc + 1) * 128], [b_h], [b_psT[s]])
                CP("act", hT[:, :, i * 128:(i + 1) * 128], psT[:, s, :].rearrange("p (k t) -> p k t", k=8), [b_psT[s]], [b_hT])

        POS_A = 0
        XW = T + 4
        POS_B = POS_A + 4 * XW // 2
        POS_C = POS_B + 4 * T // 2
        POS_D = POS_C + 4 * T // 2
        POS_E = POS_D + 4 * T // 2
        POS_F = POS_E + T // 2
        POS_S = POS_F + 16 * 192 // 2
        dwt = R[:, 47680:48704].bitcast(BF16).rearrange("p (a b) -> p a b", a=16)
        for ct_ in range(4):
            for k_ in range(4):
                TS("dve", dwt[:, ct_ * 4 + k_, :], ident32[:], cp[:, ct_, k_:k_ + 1], None, ALU.mult, None, [b_const], [b_const])
        deferred = []

        def run_unit(u, mode):
            sample = mode != "prompt"
            bar = P.barrier_deps()
            rpos[0] = POS_A
            xbr, b_xbr = carve([128, 4, XW], BF16, bar)
            yg, b_yg = carve([128, 4, T], BF16, bar)
            recT, b_rec = carve([128, 4, T], BF16, bar)
            qT, b_qT = carve([128, 4, T], BF16, bar)
            kT, b_kT = carve([128, T], BF16, bar)
            Vb, b_V = carve([128, 16, 192], BF16, bar)
            assert rpos[0] == POS_S

            def dbl(shape, dt, init, n=2):
                out = []
                for _ in range(n):
                    out.append(carve(shape, dt, init))
                return [o[0] for o in out], [o[1] for o in out]

            if mode == "sample_post":
                DMA("sp", xbr.rearrange("p c t -> p (c t)"), xbr_st, (), [b_xbr], sem="ld_x", extra=stash_ops)
                DMA("sp", yg.rearrange("p c t -> p (c t)"), yg_st, (), [b_yg], sem="ld_y", extra=stash_ops)
            else:
                rpos[0] = POS_C
                wa, b_wa = carve([128, 8, 1024], BF16, bar)
                DMA("sp", wa, w_in16[:, 768:1792].rearrange("(k p) n -> p k n", p=128), (), [b_wa], sem="wa", extra=w_in16_ready)
                xbs, b_xbs = dbl([128, 4, D], F32, bar)
                hbs, b_hbs = dbl([128, 4, D], BF16, bar)
                hTs, b_hTs = dbl([128, 8, 512], BF16, bar)
                sss, b_sss = dbl([128, 8], F32, bar)
                ysbs, b_ysbs = dbl([128, 512], F32, bar)
                y2s, b_y2s = dbl([128, 512], F32, bar)
                pctr = 0
                if sample:
                    MS("pool", xbs[1][:, 0, :], 0.0, [b_xbs[1]])
                    DMA("sp", xbs[1][0:4, 0, :], xh[:, :], (), [b_xbs[1]], sem="xb1")
                    norm_block(xbs[1], b_xbs[1], 1, 0, hbs[1], b_hbs[1], sss[1], sss[1][:, 4:8], b_sss[1])
                    transpose_tiles(hbs[1], b_hbs[1], 1, hTs[1], b_hTs[1])
                    for ct in range(4):
                        bk = pctr % 6
                        pctr += 1
                        for kc in range(8):
                            MM(ps[:, bk, 0:128], wa[:, kc, ct * 128:(ct + 1) * 128], hTs[1][:, kc, 0:128], kc == 0, kc == 7, [b_wa, b_hTs[1]], [b_ps[bk]])
                        CP("dve", xbr[:, ct, 0:2], ps[:, bk, 0:2], [b_ps[bk]], [b_xbr])
                        CP("dve", xbr[:, ct, T + 2:T + 4], ps[:, bk, 2:4], [b_ps[bk]], [b_xbr])
                else:
                    MS("pool", xbr[:, :, 0:2], 0.0, [b_xbr])
                    MS("pool", xbr[:, :, T + 2:T + 4], 0.0, [b_xbr])

                def load_norm(blk, g_idx=0):
                    k = blk % 2
                    t0 = u * T + blk * 512
                    DMA("sp", xbs[k], x[t0:t0 + 512, :].rearrange("(n p) d -> p n d", p=128), (), [b_xbs[k]], sem="xb%d" % k)
                    norm_block(xbs[k], b_xbs[k], 4, g_idx, hbs[k], b_hbs[k], sss[k], sss[k][:, 4:8], b_sss[k])

                load_norm(0)
                yctr = 0
                for blk in range(4):
                    k = blk % 2
                    transpose_tiles(hbs[k], b_hbs[k], 4, hTs[k], b_hTs[k])
                    if blk + 1 < 4:
                        load_norm(blk + 1)
                    pendB = None
                    for ft in range(8):
                        bk = pctr % 6
                        pctr += 1
                        for kc in range(8):
                            MM(ps[:, bk, :], wa[:, kc, ft * 128:(ft + 1) * 128], hTs[k][:, kc, :], kc == 0, kc == 7, [b_wa, b_hTs[k]], [b_ps[bk]])
                        if ft < 4:
                            CP("dve", xbr[:, ft, 2 + blk * 512:2 + (blk + 1) * 512], ps[:, bk, :], [b_ps[bk]], [b_xbr])
                        else:
                            ct = ft - 4
                            yk = yctr % 2
                            yctr += 1
                            ysb, b_ysb, y2, b_y2 = ysbs[yk], b_ysbs[yk], y2s[yk], b_y2s[yk]
                            CP("act", ysb, ps[:, bk, :], [b_ps[bk]], [b_ysb])
                            ACTV(y2, ps[:, bk, :], AF.Square, [b_ps[bk]], [b_y2])
                            TS("dve", y2, y2, 0.044715, 1.0, ALU.mult, ALU.add, [b_y2], [b_y2])
                            TT("dve", y2, y2, ysb, ALU.mult, [b_y2, b_ysb], [b_y2])
                            if pendB is not None:
                                pendB()
                            pendB = (lambda y2_, b_y2_, ysb_, b_ysb_, ct_, blk_: (lambda: (
                                ACTV(y2_, y2_, AF.Tanh, [b_y2_], [b_y2_], scale=GELU_K),
                                STT(yg[:, ct_, blk_ * 512:(blk_ + 1) * 512], y2_, 1.0, ysb_, ALU.add, ALU.mult, [b_y2_, b_ysb_], [b_yg]))))(
                                y2, b_y2, ysb, b_ysb, ct, blk)
                    pendB()
                if mode == "sample_pre":
                    stash_ops.append(DMA("sp", xbr_st, xbr.rearrange("p c t -> p (c t)"), [b_xbr], (), sem="st_x"))
                    stash_ops.append(DMA("sp", yg_st, yg.rearrange("p c t -> p (c t)"), [b_yg], (), sem="st_y"))

            bar = P.barrier_deps()
            b_rec.init = list(bar)
            rpos[0] = POS_D
            xcs, b_xcs = dbl([128, T], F32, bar)
            xc16s, b_xc16s = dbl([128, T], BF16, bar)
            avs, b_avs = dbl([128, 2, 1024], F32, bar)
            ivs, b_ivs = dbl([128, 2, 1024], F32, bar)
            svs, b_svs = dbl([128, 2, 1024], F32, bar)
            hf, b_hf = carve([128, T], F32, bar)
            hbk, b_hbk = carve([128, T], F32, bar)

            cvc = [0]

            def conv(ct):
                xc, b_xc = xcs[ct % 2], b_xcs[ct % 2]
                for blk in range(4):
                    bk = 6 + cvc[0] % 2
                    cvc[0] += 1
                    for k in range(4):
                        MM(ps[:, bk, :], dwt[:, ct * 4 + k, :], xbr[:, ct, blk * 512 + k:blk * 512 + k + 512], k == 0, k == 3, [b_const, b_xbr], [b_ps[bk]])
                    ACTV(xc[:, blk * 512:(blk + 1) * 512], ps[:, bk, :], AF.Identity, [b_ps[bk], b_const], [b_xc], bias=cp[:, ct, 4:5])
                    ACTV(xc16s[ct % 2][:, blk * 512:(blk + 1) * 512], ps[:, bk, :], AF.Identity, [b_ps[bk], b_const], [b_xc16s[ct % 2]], bias=cp[:, ct, 4:5])

            gstate = {"g": 0}

            def grp_bufs(g):
                s_ = g % 2
                return avs[s_], b_avs[s_], ivs[s_], b_ivs[s_], svs[s_], b_svs[s_]

            def te(g):
                ct, di = g // 2, g % 2
                xc, b_xc, xc16, b_xc16 = xcs[ct % 2], b_xcs[ct % 2], xc16s[ct % 2], b_xc16s[ct % 2]
                av, b_av, iv, b_iv, sv, b_sv = grp_bufs(g)
                halves = (0, 1) if di == 0 else (1, 0)
                for hf_i in halves:
                    tsl = slice(hf_i * 1024, (hf_i + 1) * 1024)
                    slots = []
                    for gi in range(2):
                        sl = gstate["g"] % 3
                        gstate["g"] += 1
                        slots.append(sl)
                        wsel = gw[:, (di * 2 + gi) * 4 + ct, :]
                        for sb_ in range(2):
                            MM(ps[:, 2 * sl + sb_, :], wsel, xc16[:, hf_i * 1024 + sb_ * 512: hf_i * 1024 + (sb_ + 1) * 512], True, True,
                               [b_const, b_xc16], [b_ps[2 * sl + sb_]])
                    zr = ps[:, 2 * slots[0]:2 * slots[0] + 2, :].rearrange("p a b -> p (a b)")
                    zi = ps[:, 2 * slots[1]:2 * slots[1] + 2, :].rearrange("p a b -> p (a b)")
                    br = [b_ps[2 * slots[0]], b_ps[2 * slots[0] + 1]]
                    bi = [b_ps[2 * slots[1]], b_ps[2 * slots[1] + 1]]
                    if mode == "sample_pre":
                        ACTV(av[:, hf_i, :], zr, AF.Tanh, br + [b_const], [b_av, b_summ], bias=hb[:, ct, di:di + 1], scale=0.5,
                             accum_out=sth[:, ct, di, hf_i:hf_i + 1])
                    else:
                        ACTV(av[:, hf_i, :], zr, AF.Tanh, br + [b_const], [b_av], bias=hb[:, ct, di:di + 1], scale=0.5)
                    ACTV(av[:, hf_i, :], av[:, hf_i, :], AF.Exp, [b_av, b_const], [b_av], bias=hc[:, ct, di:di + 1], scale=hc[:, ct, di:di + 1])
                    ACTV(iv[:, hf_i, :], zi, AF.Tanh, bi + [b_const], [b_iv], bias=hb[:, ct, 2 + di:3 + di], scale=0.5)
                    TT("pool", sv[:, hf_i, :], av[:, hf_i, :], av[:, hf_i, :], ALU.mult, [b_av], [b_sv])
                    STT(iv[:, hf_i, :], iv[:, hf_i, :], 1.0, xc[:, tsl], ALU.add, ALU.mult, [b_iv, b_xc], [b_iv])

            def sqs(g):
                ct, di = g // 2, g % 2
                av, b_av, iv, b_iv, sv, b_sv = grp_bufs(g)
                halves = (0, 1) if di == 0 else (1, 0)
                ACTV(sv, sv, AF.Sqrt, [b_sv], [b_sv], bias=1.0, scale=-1.0)
                for n_, hf_i in enumerate(halves):
                    tsl = slice(hf_i * 1024, (hf_i + 1) * 1024)
                    STT(iv[:, hf_i, :], iv[:, hf_i, :], 0.5, sv[:, hf_i, :], ALU.mult, ALU.mult, [b_iv, b_sv], [b_iv])
                    hdst, b_hd = (hf, b_hf) if di == 0 else (hbk, b_hbk)
                    if n_ == 0:
                        if mode == "sample_post":
                            init, rd = carry[:, ct, di:di + 1], [b_carry]
                        else:
                            init, rd = 0.0, []
                    else:
                        init = hdst[:, 1023:1024] if di == 0 else hdst[:, 1024:1025]
                        rd = []
                    if di == 0:
                        o_, a_, u_ = hdst[:, tsl], av[:, hf_i, :], iv[:, hf_i, :]
                    else:
                        o_, a_, u_ = hdst[:, tsl][:, ::-1], av[:, hf_i, :][:, ::-1], iv[:, hf_i, :][:, ::-1]
                    OP("dve", (lambda o, a__, u__, i_: (lambda e: e.tensor_tensor_scan(out=o, data0=a__, data1=u__, initial=i_, op0=ALU.mult, op1=ALU.add)))(
                        o_, a_, u_, init), [b_av, b_iv] + rd, [b_hd])

            def fin_ct(ct):
                if mode == "sample_pre":
                    for di in range(2):
                        c0 = ct * 4 + 2 * di
                        TT("dve", summ[:, c0:c0 + 1], sth[:, ct, di, 0:1], sth[:, ct, di, 1:2], ALU.add, [b_summ], [b_summ])
                        TS("dve", summ[:, c0:c0 + 1], summ[:, c0:c0 + 1], float(T), None, ALU.add, None, [b_summ], [b_summ])
                        ACTV(summ[:, c0:c0 + 1], summ[:, c0:c0 + 1], AF.Exp, [b_summ, b_const], [b_summ], scale=hc[:, ct, di:di + 1])
                    CP("dve", summ[:, ct * 4 + 1:ct * 4 + 2], hf[:, T - 1:T], [b_hf], [b_summ])
                    CP("dve", summ[:, ct * 4 + 3:ct * 4 + 4], hbk[:, 0:1], [b_hbk], [b_summ])
                else:
                    TT("pool", hf, hf, hbk, ALU.add, [b_hf, b_hbk], [b_hf])
                    STT(recT[:, ct, :], hf, 0.5, yg[:, ct, :], ALU.mult, ALU.mult, [b_hf, b_yg], [b_rec])

            conv(0)
            te(0)
            for g in range(8):
                if g + 1 < 8:
                    if (g + 1) % 2 == 0:
                        conv((g + 1) // 2)
                    te(g + 1)
                sqs(g)
                if g % 2 == 1:
                    fin_ct(g // 2)

            bar = P.barrier_deps()
            for b_ in (b_qT, b_kT, b_V):
                b_.init = list(bar)
            if mode == "sample_post":
                DMA("sp", qT.rearrange("p c t -> p (c t)"), q_st, (), [b_qT], sem="ld_q", extra=stash_ops)
            else:
                rpos[0] = POS_A
                hbs, b_hbs = dbl([128, 4, D], BF16, bar)
                assert rpos[0] <= POS_C
                rpos[0] = POS_S
                xbs, b_xbs = dbl([128, 4, D], F32, bar)
                wq, b_wq = carve([128, 8, 768], BF16, bar)
                DMA("sp", wq, w_in16[:, 0:768].rearrange("(k p) n -> p k n", p=128), (), [b_wq], sem="wq", extra=w_in16_ready)
                hTs, b_hTs = dbl([128, 8, 512], BF16, bar)
                sss, b_sss = dbl([128, 8], F32, bar)
                sqs, b_sqs = dbl([128, 640], F32, bar, 3)
                t1s, b_t1s = dbl([128, 640], F32, bar, 3)
                t2s, b_t2s = dbl([128, 640], F32, bar, 3)
                qk16s, b_qk16s = dbl([128, 640], BF16, bar, 3)
                s10s, b_s10s = dbl([128, 24], F32, bar, 3)
                MS("pool", Vb, 0.0, [b_V])
                MS("pool", Vb[:, :, 64:65], 1.0, [b_V])
                pctr = 0

                def mm_qkv(tile):
                    k = (tile // 4) % 2
                    i = tile % 4
                    sl = tile % 3
                    bq, bkv = 2 * sl, 2 * sl + 1
                    for kc in range(8):
                        MM(ps[:, bq, :], hTs[k][:, kc, i * 128:(i + 1) * 128], wq[:, kc, 0:512], kc == 0, kc == 7, [b_hTs[k], b_wq], [b_ps[bq]])
                    for kc in range(8):
                        MM(ps[:, bkv, 0:256], hTs[k][:, kc, i * 128:(i + 1) * 128], wq[:, kc, 512:768], kc == 0, kc == 7, [b_hTs[k], b_wq], [b_ps[bkv]])

                def front_qkv(tile):
                    d_ = tile % 3
                    sq, b_sq, t1, b_t1, t2, b_t2 = sqs[d_], b_sqs[d_], t1s[d_], b_t1s[d_], t2s[d_], b_t2s[d_]
                    qk16, b_qk16, s10, b_s10 = qk16s[d_], b_qk16s[d_], s10s[d_], b_s10s[d_]
                    r10 = s10[:, 12:22]
                    sl = tile % 3
                    bq, bkv = 2 * sl, 2 * sl + 1
                    qkp = ps[:, bq:bq + 2, :].rearrange("p a b -> p (a b)")[:, 0:640]
                    bb = [b_ps[bq], b_ps[bkv]]
                    ACTV(sq, qkp, AF.Square, bb, [b_sq])
                    OP("dve", (lambda o, i_: (lambda e: e.tensor_reduce(out=o, in_=i_, op=ALU.add, axis=AX.X)))(
                        s10[:, 0:10], sq.rearrange("p (h d) -> p h d", d=64)), [b_sq], [b_s10])
                    TS("dve", s10[:, 0:10], s10[:, 0:10], 1.0 / 64, EPS, ALU.mult, ALU.add, [b_s10], [b_s10])
                    TT("pool", r10, s10[:, 0:10], negh[:, 0:10], ALU.pow, [b_s10, b_const], [b_s10])
                    for (lo, nh, tq) in ((0, 8, 0), (512, 2, 2)):
                        xin = qkp[:, lo:lo + nh * 64]
                        gc = rtab[:, tq, tile, :].rearrange("p (a f j) -> p a f j", a=2, f=2, j=16).unsqueeze(1).to_broadcast([128, nh, 2, 2, 16])
                        gs = rtab[:, tq + 1, tile, :].rearrange("p (a f j) -> p a f j", a=2, f=2, j=16).unsqueeze(1).to_broadcast([128, nh, 2, 2, 16])
                        TT("dve", v5(t1[:, lo:lo + nh * 64], nh), v5(xin, nh), gc, ALU.mult, bb + [b_rtab], [b_t1])
                        TT("dve", v5(t2[:, lo:lo + nh * 64], nh), v5(xin, nh)[:, :, :, ::-1, :], gs, ALU.mult, bb + [b_rtab], [b_t2])
                    TT("dve", t1, t1, t2, ALU.add, [b_t1, b_t2], [b_t1])
                    TT("dve", qk16.rearrange("p (h d) -> p h d", d=64), t1.rearrange("p (h d) -> p h d", d=64),
                       r10.unsqueeze(2).to_broadcast([128, 10, 64]), ALU.mult, [b_t1, b_s10], [b_qk16])
                    CP("dve", Vb[:, tile, :].rearrange("p (a c) -> p a c", c=64)[:, 0::2, :],
                       ps[:, bkv, 128:256].rearrange("p (a c) -> p a c", c=64), [b_ps[bkv]], [b_V])

                def back_qkv(tile):
                    d_ = tile % 3
                    qk16, b_qk16 = qk16s[d_], b_qk16s[d_]
                    s = tr_ctr[0] % 2
                    tr_ctr[0] += 1
                    for j in range(5):
                        TR(psT[:, s, j * 128:(j + 1) * 128], qk16[:, j * 128:(j + 1) * 128], [b_qk16], [b_psT[s]])
                    CP("act", qT[:, :, tile * 128:(tile + 1) * 128], psT[:, s, 0:512].rearrange("p (k t) -> p k t", k=4), [b_psT[s]], [b_qT])
                    CP("act", kT[:, tile * 128:(tile + 1) * 128], psT[:, s, 512:640], [b_psT[s]], [b_kT])

                load_norm(0)
                for blk in range(4):
                    k = blk % 2
                    transpose_tiles(hbs[k], b_hbs[k], 4, hTs[k], b_hTs[k])
                    if blk + 1 < 4:
                        load_norm(blk + 1)
                    mm_qkv(blk * 4)
                    mm_qkv(blk * 4 + 1)
                    front_qkv(blk * 4)
                    for i in range(4):
                        tile = blk * 4 + i
                        if i + 2 < 4:
                            mm_qkv(tile + 2)
                        if i + 1 < 4:
                            front_qkv(tile + 1)
                        back_qkv(tile)


            if mode == "sample_pre":
                o1 = DMA("pool", kg_in.ap(), kT, [b_kT], (), sem="g0")
                o2 = DMA("pool", vg_in.ap(), Vb.rearrange("p t c -> p (t c)"), [b_V], (), sem="g0")
                o3 = DMA("pool", sg_in.ap(), summ[:], [b_summ], (), sem="g0")
                stash_ops.append(DMA("sp", q_st, qT.rearrange("p c t -> p (c t)"), [b_qT], (), sem="st_q"))

                ccs = []
                for ci, (a_in, a_out) in enumerate(((kg_in, kg_out), (vg_in, vg_out), (sg_in, sg_out))):
                    op_ = P.add("pool", (lambda i_, o_: (lambda e: e.collective_compute(
                        "AllGather", ALU.bypass, replica_groups=[list(range(NCORES))], ins=[i_.ap().opt()], outs=[o_.ap().opt()])))(a_in, a_out),
                        [o1, o2, o3] + ccs + w_in16_ready + w_out16_ready + w_up16_ready + w_down16_ready, dma="cc")
                    ccs.append(op_)
                cc_ops.extend(ccs)
                return

            bar = P.barrier_deps()
            rpos[0] = POS_A
            attT, b_att = carve([128, 4, T], BF16, bar)
            assert rpos[0] <= POS_C
            if sample:
                rpos[0] = POS_S
                ksrc, b_ks = carve([128, NCORES * T], BF16, bar)
                vsrc, b_vs = carve([128, NCORES * 16, 192], BF16, bar)
                for r in range(NCORES):
                    DMA("sp", ksrc[:, r * T:(r + 1) * T], kg_out.ap()[r * 128:(r + 1) * 128, :], (), [b_ks], sem="kvk", extra=cc_ops)
                    DMA("sp", vsrc[:, r * 16:(r + 1) * 16, :].rearrange("p t c -> p (t c)"), vg_out.ap()[r * 128:(r + 1) * 128, :], (), [b_vs], sem="kvv", extra=cc_ops)
                nkt = NCORES * 16
            else:
                rpos[0] = POS_S
                ksrc, b_ks, vsrc, b_vs, nkt = kT, b_kT, Vb, b_V, 16
            pt, b_pt = [], []
            for i in range(3):
                a_, b_ = carve([128, 2, 512], BF16, bar)
                pt.append(a_)
                b_pt.append(b_)
            Asb, b_A = carve([128, 512], F32, bar)
            Bsb, b_B = carve([128, 512], F32, bar)
            rinv, b_rinv = carve([128, 512], F32, bar)
            MS("pool", Asb, 0.0, [b_A])

            def setup_stage4(bar4):
                wo_, b_wo_ = carve([128, 8, D], BF16, bar4)
                DMA("sp", wo_, w_out16.rearrange("(k p) n -> p k n", p=128), (), [b_wo_], sem="wo", extra=w_out16_ready)
                xbs_, b_xbs_ = dbl([128, 4, D], F32, bar4)
                xo_, b_xo_ = dbl([128, D], F32, bar4)

                def load_x_(blk):
                    k = blk % 2
                    t0_ = u * T + blk * 512
                    DMA("sp", xbs_[k], x[t0_:t0_ + 512, :].rearrange("(n p) d -> p n d", p=128), (), [b_xbs_[k]], sem="xb%d" % k)
                load_x_(0)
                load_x_(1)
                return wo_, b_wo_, xbs_, b_xbs_, xo_, b_xo_, load_x_

            if not sample:
                st4 = setup_stage4(bar)
            tiles = [(j, qb, kt) for j in range(4) for qb in range(4) for kt in range(nkt)]
            NTL = len(tiles)

            def emit_S(i):
                j, qb, kt = tiles[i]
                sl, pi = i % 2, i % 3
                qsl = slice(qb * 512, (qb + 1) * 512)
                ksl = slice(kt * 128, (kt + 1) * 128)
                MM(ps[:, 2 * sl, :], ksrc[0:64, ksl], qT[0:64, j, qsl], True, True, [b_ks, b_qT], [b_ps[2 * sl]])
                MM(ps[:, 2 * sl + 1, :], ksrc[64:128, ksl], qT[64:128, j, qsl], True, True, [b_ks, b_qT], [b_ps[2 * sl + 1]])
                ACTV(pt[pi], ps[:, 2 * sl:2 * sl + 2, :], AF.Exp, [b_ps[2 * sl], b_ps[2 * sl + 1], b_const], [b_pt[pi]], bias=nbound[:, 0:1])

            def emit_PV(i):
                j, qb, kt = tiles[i]
                pi = i % 3
                MM(ps[:, 4, :], vsrc[:, kt, 0:128], pt[pi][:, 0, :], kt == 0, kt == nkt - 1, [b_vs, b_pt[pi]], [b_ps[4]])
                MM(ps[:, 5, :], vsrc[:, kt, 64:192], pt[pi][:, 1, :], kt == 0, kt == nkt - 1, [b_vs, b_pt[pi]], [b_ps[5]])

            def emit_fin(j, qb):
                qsl = slice(qb * 512, (qb + 1) * 512)
                MM(ps[:, 6, :], selA[:], Asb, True, False, [b_const, b_A], [b_ps[6]])
                MM(ps[:, 6, :], selB[:], Bsb, False, True, [b_const, b_B], [b_ps[6]])
                OP("dve", (lambda o, i_: (lambda e: e.reciprocal(out=o, in_=i_)))(rinv, ps[:, 6, :]), [b_ps[6]], [b_rinv])
                TT("pool", attT[0:64, j, qsl], Asb[0:64, :], rinv[0:64, :], ALU.mult, [b_A, b_rinv], [b_att])
                TT("pool", attT[64:128, j, qsl], Bsb[64:128, :], rinv[64:128, :], ALU.mult, [b_B, b_rinv], [b_att])

            emit_S(0)
            pend = None
            for i in range(NTL):
                if i + 1 < NTL:
                    emit_S(i + 1)
                emit_PV(i)
                j, qb, kt = tiles[i]
                if pend is not None and (i >= pend[2] or kt == nkt - 1):
                    emit_fin(pend[0], pend[1])
                    pend = None
                if kt == nkt - 1:
                    CP("dve", Asb, ps[:, 4, :], [b_ps[4]], [b_A])
                    CP("dve", Bsb, ps[:, 5, :], [b_ps[5]], [b_B])
                    pend = (j, qb, i + 3)
            if pend is not None:
                emit_fin(pend[0], pend[1])

            if sample:
                bar = P.barrier_deps()
                rpos[0] = POS_S
                st4 = setup_stage4(bar)
            wo, b_wo, xbs, b_xbs, xo, b_xo, load_x = st4
            pctr = 0
            for blk in range(4):
                t0 = u * T + blk * 512
                if blk + 1 < 4 and blk >= 1:
                    load_x(blk + 1)
                for i in range(4):
                    tsl = slice(blk * 512 + i * 128, blk * 512 + (i + 1) * 128)
                    sl = pctr % 3
                    pctr += 1
                    for half in range(2):
                        bk = 2 * sl + half
                        for kc in range(8):
                            lhs = attT[:, kc, tsl] if kc < 4 else recT[:, kc - 4, tsl]
                            MM(ps[:, bk, :], lhs, wo[:, kc, half * 512:(half + 1) * 512], kc == 0, kc == 7, [b_att, b_rec, b_wo], [b_ps[bk]])
                    o = xo[pctr % 2]
                    bo = b_xo[pctr % 2]
                    TT("dve", o, ps[:, 2 * sl:2 * sl + 2, :].rearrange("p a b -> p (a b)"), xbs[blk % 2][:, i, :], ALU.add,
                       [b_ps[2 * sl], b_ps[2 * sl + 1], b_xbs[blk % 2]], [bo])
                    r0 = t0 + i * 128
                    x1_writes.append(DMA("sp", x1s[r0:r0 + 128, :], o, [bo], (), sem="xo%d" % (pctr % 2)))
            return

        x1_writes = []
        cc_ops = []
        stash_ops = []
        load_rope(ropeS)
        run_unit(4, "sample_pre")
        load_rope(ropeP)
        for u in range(4):
            run_unit(u, "prompt")
        bar = P.barrier_deps()
        b_g = Buf(bar)
        DMA("sp", gath[:], sg_out.ap().rearrange("(r p) x -> p r x", p=128), (), [b_g], sem="gath", extra=cc_ops)
        gv = gath[:].rearrange("p r (c k) -> p r c k", k=4)
        cv = ctmp[:].rearrange("p r (c k) -> p r c k", k=4)
        for di in range(2):
            mk = cm[:, di * 8:(di + 1) * 8].unsqueeze(2).to_broadcast([128, NCORES, 4])
            TS("dve", cv[:, :, :, 2 * di], gv[:, :, :, 2 * di], -1.0, None, ALU.add, None, [b_g], [b_g])
            TT("dve", cv[:, :, :, 2 * di], cv[:, :, :, 2 * di], mk, ALU.mult, [b_g, b_const], [b_g])
            TS("dve", cv[:, :, :, 2 * di], cv[:, :, :, 2 * di], 1.0, None, ALU.add, None, [b_g], [b_g])
            TT("dve", cv[:, :, :, 2 * di + 1], gv[:, :, :, 2 * di + 1], mk, ALU.mult, [b_g, b_const], [b_g])
            MS("dve", carry[:, :, di], 0.0, [b_carry])
            order = range(NCORES) if di == 0 else range(NCORES - 1, -1, -1)
            for r in order:
                TT("dve", carry[:, :, di], carry[:, :, di], cv[:, r, :, 2 * di], ALU.mult, [b_g, b_carry], [b_carry])
                TT("dve", carry[:, :, di], carry[:, :, di], cv[:, r, :, 2 * di + 1], ALU.add, [b_g, b_carry], [b_carry])
        run_unit(4, "sample_post")

        rpos[0] = 0
        rlim[0] = RW
        bar = P.barrier_deps()
        wu, b_wu = carve([128, 8, DFF], BF16, bar)
        wd, b_wd = carve([128, 32, D], BF16, bar)
        for kc in range(8):
            DMA("sp", wu[:, kc, :], w_up16[kc * 128:(kc + 1) * 128, :], (), [b_wu], sem="wu", extra=w_up16_ready)
        for fc in range(0, 32, 8):
            DMA("sp", wd[:, fc:fc + 8, :], w_down16[fc * 128:(fc + 8) * 128, :].rearrange("(k p) n -> p k n", p=128), (), [b_wd], sem="wd",
                extra=w_down16_ready)
        aT, b_aT = carve([128, 32, 512], BF16, bar)
        hT, b_hT = carve([128, 8, 512], BF16, bar)
        xt, b_xt, hb2, b_hb2, rl, b_rl, xr, b_xr = [], [], [], [], [], [], [], []
        xt0, b_xt0 = carve([128, D], F32, bar)
        hb20, b_hb20 = carve([128, D], BF16, bar)
        for i in range(2):
            xt.append(xt0); b_xt.append(b_xt0)
            hb2.append(hb20); b_hb2.append(b_hb20)
            a_, b_ = carve([128, 512], F32, bar); rl.append(a_); b_rl.append(b_)
            a_, b_ = carve([128, D], F32, bar); xr.append(a_); b_xr.append(b_)
        yo, b_yo = carve([128, D], F32, bar)
        sn, b_sn = carve([128, 16], F32, bar)
        out_writes = []
        NB2 = NU * T // 512
        cnt = {"n": 0, "u": 0, "f": 0}

        def p2_norm(blk):
            t0 = blk * 512
            for i in range(4):
                k = cnt["n"] % 2
                cnt["n"] += 1
                r0 = t0 + i * 128
                DMA("sp", xt[k], x1s[r0:r0 + 128, :], (), [b_xt[k]], sem="xt0", extra=[x1_writes[r0 // 128]])
                ACTV(hb2[k], xt[k], AF.Square, [b_xt[k]], [b_hb2[k], b_sn], accum_out=sn[:, 0:1])
                TS("dve", sn[:, 0:1], sn[:, 0:1], 1.0 / D, EPS, ALU.mult, ALU.add, [b_sn], [b_sn])
                TT("pool", sn[:, 1:2], sn[:, 0:1], negh[:, 0:1], ALU.pow, [b_sn, b_const], [b_sn])
                STT(hb2[k], xt[k], sn[:, 1:2], gbc[:, 1, :], ALU.mult, ALU.mult, [b_xt[k], b_sn, b_const], [b_hb2[k]])
                s_ = tr_ctr[0] % 2
                tr_ctr[0] += 1
                for kc in range(8):
                    TR(psT[:, s_, kc * 128:(kc + 1) * 128], hb2[k][:, kc * 128:(kc + 1) * 128], [b_hb2[k]], [b_psT[s_]])
                CP("act", hT[:, :, i * 128:(i + 1) * 128], psT[:, s_, :].rearrange("p (k t) -> p k t", k=8), [b_psT[s_]], [b_hT])

        def p2_up(blk):
            for ff in range(32):
                bk = cnt["u"] % 2
                cnt["u"] += 1
                for kc in range(8):
                    MM(ps[:, bk, :], wu[:, kc, ff * 128:(ff + 1) * 128], hT[:, kc, :], kc == 0, kc == 7, [b_wu, b_hT], [b_ps[bk]])
                ACTV(rl[bk], ps[:, bk, :], AF.Relu, [b_ps[bk]], [b_rl[bk]])
                TT("dve", aT[:, ff, :], ps[:, bk, :], rl[bk], ALU.mult, [b_ps[bk], b_rl[bk]], [b_aT])

        def p2_down(blk, pair):
            for i in range(2):
                ti = pair * 2 + i
                for half in range(2):
                    bk = 2 + 2 * i + half
                    for ff in range(32):
                        MM(ps[:, bk, :], aT[:, ff, ti * 128:(ti + 1) * 128], wd[:, ff, half * 512:(half + 1) * 512], ff == 0, ff == 31, [b_aT, b_wd], [b_ps[bk]])

        def p2_final(blk, pair):
            t0 = blk * 512
            for i in range(2):
                ti = pair * 2 + i
                bk = 2 + 2 * i
                k = cnt["f"] % 2
                cnt["f"] += 1
                r0 = t0 + ti * 128
                DMA("sp", xr[k], x1s[r0:r0 + 128, :], (), [b_xr[k]], sem="xr%d" % k, extra=[x1_writes[r0 // 128]])
                TT("dve", xr[k], ps[:, bk:bk + 2, :].rearrange("p a b -> p (a b)"), xr[k], ALU.add, [b_ps[bk], b_ps[bk + 1], b_xr[k]], [b_xr[k]])
                ACTV(yo, xr[k], AF.Square, [b_xr[k]], [b_yo, b_sn], accum_out=sn[:, 4:5])
                TS("dve", sn[:, 4:5], sn[:, 4:5], 1.0 / D, EPS, ALU.mult, ALU.add, [b_sn], [b_sn])
                TT("pool", sn[:, 5:6], sn[:, 4:5], negh[:, 0:1], ALU.pow, [b_sn, b_const], [b_sn])
                STT(yo, xr[k], sn[:, 5:6], gbc[:, 2, :], ALU.mult, ALU.mult, [b_xr[k], b_sn, b_const], [b_yo])
                out_writes.append(DMA("sp", y[r0:r0 + 128, :], yo, [b_yo], (), sem="yo"))

        p2_norm(0)
        for blk in range(NB2):
            p2_up(blk)
            p2_down(blk, 0)
            if blk + 1 < NB2:
                p2_norm(blk + 1)
            p2_final(blk, 0)
            p2_down(blk, 1)
            p2_final(blk, 1)

        P.finalize()
        sems = {k: es.enter_context(nc.semaphore(k.replace(":", "_"))) for k in P.sem_keys()}
        with nc.Block() as block:
            @block.sync
            def _(e):
                P.replay("sp", e, sems, out_writes)

            @block.tensor
            def _(e):
                P.replay("pe", e, sems)

            @block.vector
            def _(e):
                P.replay("dve", e, sems)

            @block.scalar
            def _(e):
                P.replay("act", e, sems)

            @block.gpsimd
            def _(e):
                P.replay("pool", e, sems)
    return nc


def _rope_table(pos0):
    t = pos0 + np.arange(T)
    row = (t // 64).astype(np.float32)
    col = (t % 64).astype(np.float32)
    inv = (10000.0 ** (-np.arange(0, 32, 2, dtype=np.float32) / 32)).astype(np.float32)
    ang = np.stack([row[:, None] * inv[None, :], col[:, None] * inv[None, :]], axis=1).astype(np.float32)
    c = np.cos(ang).astype(np.float32)
    s = np.sin(ang).astype(np.float32)
    C = np.stack([c, c], axis=2)
    S = np.stack([-s, s], axis=2)
    tab = np.stack([C.reshape(T, 64), S.reshape(T, 64)], axis=0)
    tab = tab.reshape(2, 16, 128, 64).transpose(2, 0, 1, 3)
    return np.ascontiguousarray(tab, dtype=np.float32)


_NC_CACHE = {}


def kernel(x_prompt, x_sample, norm_mix_g, w_in, q_norm_g, k_norm_g, conv_w, conv_b, lru_wa, lru_ba,
           lru_wx, lru_bx, lru_lambda, w_out, norm_mlp_g, w_up, w_down, norm_final_g):
    f = lambda a: np.ascontiguousarray(np.asarray(a), dtype=np.float32)
    x_prompt, x_sample = f(x_prompt), f(x_sample)
    w_in0 = f(w_in)[0]
    perm_heads = [4 * e + j for j in range(4) for e in range(2)]
    qcols = np.concatenate([np.arange(h * 64, (h + 1) * 64) for h in perm_heads])
    w_in_l = np.ascontiguousarray(np.concatenate([w_in0[:, qcols], w_in0[:, 512:]], axis=1))
    w_out0 = f(w_out)[0]
    w_out_l = np.ascontiguousarray(np.concatenate([w_out0[qcols, :], w_out0[512:, :]], axis=0))
    gvec = np.stack([f(norm_mix_g)[0], f(norm_mlp_g)[0], f(norm_final_g)], axis=0)
    gqk = np.stack([f(q_norm_g)[0], f(k_norm_g)[0]], axis=0)
    chan = np.concatenate([f(conv_w)[0], f(conv_b), f(lru_ba)[0], f(lru_bx)[0], f(lru_lambda)[0]], axis=0)
    cpk = np.ascontiguousarray(chan.reshape(11, 4, 128).transpose(2, 1, 0))
    ropeP = _rope_table(0)
    xs = x_sample[0]
    in_maps = []
    for c in range(NCORES):
        xc = np.concatenate([x_prompt[4 * c:4 * c + 4].reshape(4 * T, D), xs[c * T:(c + 1) * T]], axis=0)
        xh = np.zeros((4, D), np.float32)
        if c > 0:
            xh[0:2] = xs[c * T - 2:c * T]
        if c < NCORES - 1:
            xh[2] = xs[(c + 1) * T]
        cmask = np.zeros((128, 16), np.float32)
        cmask[:, 0:8] = (np.arange(8) < c).astype(np.float32)[None, :]
        cmask[:, 8:16] = (np.arange(8) > c).astype(np.float32)[None, :]
        in_maps.append({
            "x": np.ascontiguousarray(xc), "xh": xh, "w_in": w_in_l, "w_out": w_out_l, "w_up": f(w_up)[0], "w_down": f(w_down)[0],
            "gvec": gvec, "gqk": gqk, "cpk": cpk, "lwa": f(lru_wa)[0], "lwx": f(lru_wx)[0],
            "ropeP": ropeP, "ropeS": _rope_table(c * T), "cmask": cmask,
        })
    if "nc" not in _NC_CACHE:
        _NC_CACHE["nc"] = build_nc()
    res = run_bass_kernel_spmd(_NC_CACHE["nc"], in_maps, core_ids=list(range(NCORES)))
    y_prompt = np.empty((32, T, D), np.float32)
    y_sample = np.empty((1, NCORES * T, D), np.float32)
    for c in range(NCORES):
        yc = res.results[c]["y"]
        y_prompt[4 * c:4 * c + 4] = yc[:4 * T].reshape(4, T, D)
        y_sample[0, c * T:(c + 1) * T] = yc[4 * T:]
    return (y_prompt, y_sample)
```
